# Optimizing a Trainium2 kernel written in Bass

```python
import math
import jax, jax.numpy as jnp
from jax import lax
import numpy as np

D_MODEL = 2048
BATCH = 2
SEQ = 4096
DEPTH = 2
DEC_BATCH = 8
DEC_SEQ = 4
PAST_LEN = 16384
PAGE_SIZE = 128

N_EVEN = (DEPTH + 1) // 2
N_ODD = DEPTH // 2
A_DK = 128
A_DV = 128
A_HEADS = (5 * D_MODEL // 8) // A_DK
A_WIDTH = A_HEADS * A_DK
A_CONV = 4
A_CHUNK = 64
B_CONFIGS = ((128, 1), (512, 4), (2048, 16))
B_NGROUPS = len(B_CONFIGS)
B_HD = 64
B_HPG = (D_MODEL - A_WIDTH) // (B_NGROUPS * B_HD)
B_WIDTH = B_NGROUPS * B_HPG * B_HD
B_QBLOCK = 128
B_SCALE = B_HD ** -0.5
C_WIDTH = D_MODEL
C_CONV = 3
FFN_DIM = 5632
N_MOD = 9
EPS = 1e-6
IN0_DIM = 4 * A_WIDTH + 2 * A_HEADS + 3 * B_WIDTH
MIX0_DIM = A_WIDTH + B_WIDTH

kernel_name = "hybrid_gdn_dilswa_shortconv_step"


def rms_norm(x, gain):
    xf = x.astype(jnp.float32)
    xf = xf * lax.rsqrt(jnp.mean(xf * xf, axis=-1, keepdims=True) + EPS)
    return xf.astype(x.dtype) * gain


def l2_normalize(x):
    xf = x.astype(jnp.float32)
    return xf * lax.rsqrt(jnp.sum(xf * xf, axis=-1, keepdims=True) + EPS)


def modulate(h, shift, scale):
    return h * (1.0 + scale[:, None, :]) + shift[:, None, :]


def swiglu(h, w_up, w_down):
    g, u = jnp.split(h @ w_up, 2, axis=-1)
    return (jax.nn.silu(g) * u) @ w_down


def causal_dwconv(x, buf, w):
    width = w.shape[0]
    L = x.shape[1]
    xp = jnp.concatenate([buf, x], axis=1)
    y = xp[:, 0:L] * w[0]
    for j in range(1, width):
        y = y + xp[:, j:j + L] * w[j]
    return y, xp[:, L:]


def gated_delta_chunked(q, k, v, g, beta, S0, chunk):
    f32 = jnp.float32
    bsz, L, H, K = q.shape
    V = v.shape[-1]
    n = L // chunk

    def blk(t):
        t = t.astype(f32).reshape((bsz, n, chunk, H) + t.shape[3:])
        return jnp.moveaxis(t, (1, 3), (0, 2))

    qc, kc, vc, gc, bc = blk(q), blk(k), blk(v), blk(g), blk(beta)
    gam = jnp.cumsum(gc, axis=-1)
    idx = jnp.arange(chunk)
    causal = idx[:, None] >= idx[None, :]
    strict = idx[:, None] > idx[None, :]
    diff = gam[..., :, None] - gam[..., None, :]
    decay = jnp.where(causal, jnp.exp(jnp.where(causal, diff, 0.0)), 0.0)
    kk = jnp.einsum('nbhik,nbhjk->nbhij', kc, kc)
    tri = jnp.where(strict, bc[..., :, None] * kk * decay, 0.0) + jnp.eye(chunk, dtype=f32)
    rhs = jnp.concatenate([vc * bc[..., None], kc * (bc * jnp.exp(gam))[..., None]], axis=-1)
    sol = lax.linalg.triangular_solve(tri, rhs, left_side=True, lower=True, unit_diagonal=True)
    u_val, w_key = sol[..., :V], sol[..., V:]
    qk = jnp.einsum('nbhik,nbhjk->nbhij', qc, kc) * decay
    q_dec = qc * jnp.exp(gam)[..., None]
    k_dec = kc * jnp.exp(gam[..., -1:] - gam)[..., None]
    g_tot = jnp.exp(gam[..., -1])

    def step(S, xs):
        u_i, w_i, qk_i, qd_i, kd_i, gt_i = xs
        u = u_i - jnp.einsum('bhck,bhkv->bhcv', w_i, S)
        o = jnp.einsum('bhck,bhkv->bhcv', qd_i, S) + jnp.einsum('bhij,bhjv->bhiv', qk_i, u)
        S = S * gt_i[..., None, None] + jnp.einsum('bhck,bhcv->bhkv', kd_i, u)
        return S, o

    S_fin, o = lax.scan(step, S0.astype(f32), (u_val, w_key, qk, q_dec, k_dec, g_tot))
    o = jnp.moveaxis(o, (0, 2), (1, 3)).reshape(bsz, L, H, V)
    return o, S_fin


def dilated_swa(q, k, v, kv_pasts, pos0):
    bsz, L = q.shape[:2]
    qb = math.gcd(L, B_QBLOCK)
    starts = jnp.arange(0, L, qb)
    outs, lses, new_kv = [], [], []
    for g, (win, dil) in enumerate(B_CONFIGS):
        qg = q[:, :, g]
        kv_new = jnp.stack([k[:, :, g], v[:, :, g]], axis=2)
        kv_past = kv_pasts[g]
        n_past = kv_past.shape[1]
        pad = jnp.zeros((bsz, win - n_past) + kv_new.shape[2:], kv_new.dtype)
        kv_full = jnp.concatenate([pad, kv_past, kv_new], axis=1)
        n_keys = win // dil + 1
        qi = jnp.arange(qb)[:, None]
        mi = jnp.arange(n_keys)[None, :]
        loc = win + qi - dil * mi

        def block(i0):
            kv_blk = lax.dynamic_slice_in_dim(kv_full, i0, win + qb, axis=1)
            kv_sel = kv_blk[:, loc]
            q_blk = lax.dynamic_slice_in_dim(qg, i0, qb, axis=1)
            s = jnp.einsum('bqhd,bqmhd->bhqm', q_blk, kv_sel[:, :, :, 0]).astype(jnp.float32) * B_SCALE
            valid = (pos0 + i0 + qi - dil * mi) >= 0
            s = jnp.where(valid, s, -jnp.inf)
            smax = jnp.max(s, axis=-1, keepdims=True)
            p = jnp.exp(s - smax)
            den = jnp.sum(p, axis=-1, keepdims=True)
            o = jnp.einsum('bhqm,bqmhd->bqhd', (p / den).astype(qg.dtype), kv_sel[:, :, :, 1])
            return o, (smax + jnp.log(den))[..., 0]

        o, lse = lax.map(block, starts)
        outs.append(jnp.moveaxis(o, 0, 1).reshape(bsz, L, B_HPG, B_HD))
        lses.append(jnp.transpose(lse, (1, 0, 3, 2)).reshape(bsz, L, B_HPG))
        keep = n_past if n_past > 0 else min(win, L)
        new_kv.append(jnp.concatenate([kv_past, kv_new], axis=1)[:, -keep:])
    alpha = jax.nn.softmax(jnp.stack(lses, axis=0), axis=0)
    y = jnp.concatenate([outs[g] * alpha[g][..., None].astype(outs[g].dtype) for g in range(B_NGROUPS)], axis=2)
    return y.reshape(bsz, L, B_WIDTH), new_kv


def even_mixer(h, pos0, S0, conv_buf, kv_pasts, w_in, conv_w, a_log, dt_bias, onorm, w_out):
    bsz, L, _ = h.shape
    proj = h @ w_in
    qkv_a, z_a, a_a, b_a, qkv_b = jnp.split(
        proj, [3 * A_WIDTH, 4 * A_WIDTH, 4 * A_WIDTH + A_HEADS, 4 * A_WIDTH + 2 * A_HEADS], axis=-1)
    qkv_a, new_conv = causal_dwconv(qkv_a, conv_buf, conv_w)
    qkv_a = jax.nn.silu(qkv_a).reshape(bsz, L, 3, A_HEADS, A_DK)
    q_a = l2_normalize(qkv_a[:, :, 0]) * (A_DK ** -0.5)
    k_a = l2_normalize(qkv_a[:, :, 1])
    v_a = qkv_a[:, :, 2]
    beta = jax.nn.sigmoid(b_a.astype(jnp.float32))
    g_log = -jnp.exp(a_log.astype(jnp.float32)) * jax.nn.softplus(a_a.astype(jnp.float32) + dt_bias.astype(jnp.float32))
    o_a, S_new = gated_delta_chunked(q_a, k_a, v_a, g_log, beta, S0, math.gcd(L, A_CHUNK))
    o_a = rms_norm(o_a.astype(h.dtype), onorm) * jax.nn.silu(z_a.reshape(bsz, L, A_HEADS, A_DV))
    qkv_b = qkv_b.reshape(bsz, L, 3, B_NGROUPS, B_HPG, B_HD)
    o_b, new_kv = dilated_swa(qkv_b[:, :, 0], qkv_b[:, :, 1], qkv_b[:, :, 2], kv_pasts, pos0)
    mix = jnp.concatenate([o_a.reshape(bsz, L, A_WIDTH), o_b], axis=-1) @ w_out
    return mix, S_new.astype(S0.dtype), new_conv, new_kv


def short_conv_mixer(h, conv_buf, w_in, conv_w, w_out):
    bg, cg, xt = jnp.split(h @ w_in, 3, axis=-1)
    y, new_buf = causal_dwconv(cg * xt, conv_buf, conv_w)
    return (bg * y) @ w_out, new_buf


def trunk(x, c, pos0, st_S, st_conv, st_kv, st_sc, p):
    new_S, new_conv, new_sc = [], [], []
    new_kv = [[] for _ in range(B_NGROUPS)]
    for l in range(DEPTH):
        mod = jax.nn.silu(c) @ p['ada_w'][l] + p['ada_b'][l]
        sh1, sc1, g1, sh2, sc2, g2, sh3, sc3, g3 = jnp.split(mod, N_MOD, axis=-1)
        h = modulate(rms_norm(x, p['ln_ffn1'][l]), sh1, sc1)
        x = x + 0.5 * g1[:, None, :] * swiglu(h, p['ffn_w_up'][l, 0], p['ffn_w_down'][l, 0])
        h = modulate(rms_norm(x, p['ln_mix'][l]), sh2, sc2)
        if l % 2 == 0:
            e = l // 2
            mix, S_e, conv_e, kv_e = even_mixer(
                h, pos0, st_S[e], st_conv[e], [kv[e] for kv in st_kv],
                p['w_in0'][e], p['gdn_conv_w'][e], p['gdn_a_log'][e], p['gdn_dt_bias'][e],
                p['gdn_onorm'][e], p['w_out0'][e])
            new_S.append(S_e)
            new_conv.append(conv_e)
            for gi in range(B_NGROUPS):
                new_kv[gi].append(kv_e[gi])
        else:
            o = l // 2
            mix, sc_o = short_conv_mixer(h, st_sc[o], p['sc_w_in'][o], p['sc_conv_w'][o], p['sc_w_out'][o])
            new_sc.append(sc_o)
        x = x + g2[:, None, :] * mix
        h = modulate(rms_norm(x, p['ln_ffn2'][l]), sh3, sc3)
        x = x + 0.5 * g3[:, None, :] * swiglu(h, p['ffn_w_up'][l, 1], p['ffn_w_down'][l, 1])
    y = rms_norm(x, p['ln_final'])
    return (y, jnp.stack(new_S), jnp.stack(new_conv), jnp.stack(new_kv[0]), jnp.stack(new_kv[1]),
            jnp.stack(new_kv[2]), jnp.stack(new_sc))


def setup_inputs(seed: int = 0) -> dict:
    key = jax.random.key(seed)
    ks = iter(jax.random.split(key, 40))
    f32 = jnp.float32

    def nrm(shape, scale):
        return jax.random.normal(next(ks), shape, f32) * scale

    D = D_MODEL
    dt = jnp.exp(jax.random.uniform(next(ks), (N_EVEN, A_HEADS), f32, math.log(1e-3), math.log(1e-1)))
    return {
        'x_prompt': nrm((BATCH, SEQ, D), 1.0),
        'x_sample': nrm((DEC_BATCH, DEC_SEQ, D), 1.0),
        'c_prompt': nrm((BATCH, D), 1.0),
        'c_sample': nrm((DEC_BATCH, D), 1.0),
        'state_gdn_S': nrm((N_EVEN, DEC_BATCH, A_HEADS, A_DK, A_DV), 0.5),
        'state_gdn_conv': nrm((N_EVEN, DEC_BATCH, A_CONV - 1, 3 * A_WIDTH), 1.0),
        'cache_swa_kv_g0': nrm((N_EVEN, DEC_BATCH, min(B_CONFIGS[0][0], PAST_LEN), 2, B_HPG, B_HD), 1.0),
        'cache_swa_kv_g1': nrm((N_EVEN, DEC_BATCH, min(B_CONFIGS[1][0], PAST_LEN), 2, B_HPG, B_HD), 1.0),
        'cache_swa_kv_g2': nrm((N_EVEN, DEC_BATCH, min(B_CONFIGS[2][0], PAST_LEN), 2, B_HPG, B_HD), 1.0),
        'state_sconv': nrm((N_ODD, DEC_BATCH, C_CONV - 1, C_WIDTH), 1.0),
        'ada_w': nrm((DEPTH, D, N_MOD * D), D ** -0.5),
        'ada_b': nrm((DEPTH, N_MOD * D), 0.02),
        'ln_ffn1': 1.0 + nrm((DEPTH, D), 0.05),
        'ln_mix': 1.0 + nrm((DEPTH, D), 0.05),
        'ln_ffn2': 1.0 + nrm((DEPTH, D), 0.05),
        'ln_final': 1.0 + nrm((D,), 0.05),
        'ffn_w_up': nrm((DEPTH, 2, D, 2 * FFN_DIM), D ** -0.5),
        'ffn_w_down': nrm((DEPTH, 2, FFN_DIM, D), FFN_DIM ** -0.5),
        'w_in0': nrm((N_EVEN, D, IN0_DIM), D ** -0.5),
        'gdn_conv_w': nrm((N_EVEN, A_CONV, 3 * A_WIDTH), A_CONV ** -0.5),
        'gdn_a_log': jnp.log(jax.random.uniform(next(ks), (N_EVEN, A_HEADS), f32, 1.0, 16.0)),
        'gdn_dt_bias': dt + jnp.log(-jnp.expm1(-dt)),
        'gdn_onorm': 1.0 + nrm((N_EVEN, A_DV), 0.05),
        'w_out0': nrm((N_EVEN, MIX0_DIM, D), MIX0_DIM ** -0.5),
        'sc_w_in': nrm((N_ODD, D, 3 * C_WIDTH), D ** -0.5),
        'sc_conv_w': nrm((N_ODD, C_CONV, C_WIDTH), C_CONV ** -0.5),
        'sc_w_out': nrm((N_ODD, C_WIDTH, D), C_WIDTH ** -0.5),
    }


def reference(x_prompt, x_sample, c_prompt, c_sample, state_gdn_S, state_gdn_conv,
              cache_swa_kv_g0, cache_swa_kv_g1, cache_swa_kv_g2, state_sconv,
              ada_w, ada_b, ln_ffn1, ln_mix, ln_ffn2, ln_final, ffn_w_up, ffn_w_down,
              w_in0, gdn_conv_w, gdn_a_log, gdn_dt_bias, gdn_onorm, w_out0,
              sc_w_in, sc_conv_w, sc_w_out):
    params = {
        'ada_w': ada_w, 'ada_b': ada_b, 'ln_ffn1': ln_ffn1, 'ln_mix': ln_mix,
        'ln_ffn2': ln_ffn2, 'ln_final': ln_final, 'ffn_w_up': ffn_w_up, 'ffn_w_down': ffn_w_down,
        'w_in0': w_in0, 'gdn_conv_w': gdn_conv_w, 'gdn_a_log': gdn_a_log,
        'gdn_dt_bias': gdn_dt_bias, 'gdn_onorm': gdn_onorm, 'w_out0': w_out0,
        'sc_w_in': sc_w_in, 'sc_conv_w': sc_conv_w, 'sc_w_out': sc_w_out,
    }
    dt = x_prompt.dtype
    bp = x_prompt.shape[0]
    p_S0 = jnp.zeros((N_EVEN, bp, A_HEADS, A_DK, A_DV), dt)
    p_conv0 = jnp.zeros((N_EVEN, bp, A_CONV - 1, 3 * A_WIDTH), dt)
    p_kv0 = [jnp.zeros((N_EVEN, bp, 0, 2, B_HPG, B_HD), dt) for _ in range(B_NGROUPS)]
    p_sc0 = jnp.zeros((N_ODD, bp, C_CONV - 1, C_WIDTH), dt)
    (y_prompt, p_S, p_conv, p_kv_g0, p_kv_g1, p_kv_g2, p_sc) = trunk(
        x_prompt, c_prompt, 0, p_S0, p_conv0, p_kv0, p_sc0, params)
    (y_sample, s_S, s_conv, s_kv_g0, s_kv_g1, s_kv_g2, s_sc) = trunk(
        x_sample, c_sample, PAST_LEN, state_gdn_S, state_gdn_conv,
        [cache_swa_kv_g0, cache_swa_kv_g1, cache_swa_kv_g2], state_sconv, params)
    return (y_prompt, y_sample, p_S, p_conv, p_kv_g0, p_kv_g1, p_kv_g2, p_sc,
            s_S, s_conv, s_kv_g0, s_kv_g1, s_kv_g2, s_sc)
```

```python
import contextlib
import numpy as np
import concourse.bass as bass
import concourse.mybir as mybir
from concourse.bass_utils import run_bass_kernel_spmd

F32 = mybir.dt.float32
BF16 = mybir.dt.bfloat16
AF = mybir.ActivationFunctionType
ALU = mybir.AluOpType

D = 2048
KD = 16
FF = 5632
FC = 44
LP = 4096
LS = 4
TT = 512
EPS = 1e-6

EPOCH = 12000
NDS = 12
ENGS = ("pe", "act", "dve", "pool", "sp")
SCHEDULE = True
SAME_ENGINE_SYNC = {"pe": False, "act": True, "dve": True, "pool": True, "sp": False}


class Buf:
    __slots__ = ("name", "w", "r")

    def __init__(self, name):
        self.name = name
        self.w = None
        self.r = []


class Op:
    __slots__ = ("eng", "idx", "fn", "deps", "is_dma", "ref", "semval", "dslot", "dval", "seq", "cost", "lat", "users",
                 "nrem", "ready", "fin")

    def __init__(self, eng, idx, fn, is_dma):
        self.eng, self.idx, self.fn, self.is_dma = eng, idx, fn, is_dma
        self.seq = 0
        self.cost = 0.3
        self.lat = 0.0
        self.users = []
        self.nrem = 0
        self.ready = 0.0
        self.fin = 0.0
        self.deps = []
        self.ref = False
        self.semval = None
        self.dslot = None
        self.dval = None


class FW:
    def __init__(self, nc):
        self.nc = nc
        self.ops = {e: [] for e in ENGS}
        self.ndma = {e: 0 for e in ENGS}
        self.stack = contextlib.ExitStack()
        self.nbuf = 0
        self.nseq = 0
        self.allbufs = []
        self.last_fence = None

    def sbuf(self, name, shape, dt):
        return self.stack.enter_context(self.nc.sbuf_tensor(name, list(shape), dt))

    def psum(self, name, shape, dt=F32):
        return self.stack.enter_context(self.nc.psum_tensor(name, list(shape), dt))

    def buf(self, name=None):
        self.nbuf += 1
        b = Buf(name or f"b{self.nbuf}")
        b.w = self.last_fence
        self.allbufs.append(b)
        return b

    def fence(self, scratch_ap):
        self.last_fence = self.op("dve", lambda e: e.memset(scratch_ap, 0.0), reads=(), writes=list(self.allbufs))

    def op(self, eng, fn, reads=(), writes=(), dma=False, cost=None, lat=0.0):
        lst = self.ops[eng]
        o = Op(eng, len(lst), fn, dma)
        self.nseq += 1
        o.seq = self.nseq
        o.cost = cost if cost is not None else (0.06 if dma else 0.3)
        o.lat = lat if lat else (3.0 if dma else 0.0)
        deps = []
        for b in reads:
            if b.w is not None:
                deps.append(b.w)
        for b in writes:
            if b.w is not None:
                deps.append(b.w)
            deps.extend(b.r)
        o.deps = deps
        if dma:
            self.ndma[eng] += 1
        lst.append(o)
        for b in writes:
            b.w = o
            b.r = []
        for b in reads:
            if b.w is not o:
                b.r.append(o)
        return o

    def pe(self, fn, reads=(), writes=(), cost=None):
        return self.op("pe", fn, reads, writes, cost=cost)

    def act(self, fn, reads=(), writes=(), cost=None):
        return self.op("act", fn, reads, writes, cost=cost)

    def dve(self, fn, reads=(), writes=(), cost=None):
        return self.op("dve", fn, reads, writes, cost=cost)

    def dma(self, q, out, in_, reads=(), writes=(), **kw):
        nb = 1
        for x in out.shape:
            nb *= x
        lat = 2.5 + nb * 4 / 150e3
        return self.op(q, lambda e: e.dma_start(out=out, in_=in_, **kw), reads, writes, dma=True, lat=lat)

    def schedule(self):
        import heapq
        allops = []
        for e in ENGS:
            allops.extend(self.ops[e])
        for o in allops:
            o.users = []
        for o in allops:
            ds = set(id(d) for d in o.deps)
            uniq = {}
            for d in o.deps:
                uniq[id(d)] = d
            o.deps = list(uniq.values())
            o.nrem = len(o.deps)
            for d in o.deps:
                d.users.append(o)
        readyq = {e: [] for e in ENGS}
        free_at = {e: 0.0 for e in ENGS}
        busy = {e: False for e in ENGS}
        order = {e: [] for e in ENGS}
        events = []
        SYNC = 0.15

        def try_start(e, now):
            if busy[e] or not readyq[e]:
                return
            seq, _, o = heapq.heappop(readyq[e])
            st = max(now, free_at[e], o.ready)
            o.fin = st + o.cost
            free_at[e] = o.fin
            busy[e] = True
            order[e].append(o)
            heapq.heappush(events, (o.fin, o.seq, 0, o))
            heapq.heappush(events, (o.fin + o.lat, o.seq, 1, o))

        for o in allops:
            if o.nrem == 0:
                heapq.heappush(readyq[o.eng], (o.seq, id(o), o))
        for e in ENGS:
            try_start(e, 0.0)
        nsched = 0
        while events:
            t, _, kind, o = heapq.heappop(events)
            if kind == 0:
                busy[o.eng] = False
                try_start(o.eng, t)
            else:
                nsched += 1
                for u in o.users:
                    u.nrem -= 1
                    if u.ready < t + SYNC:
                        u.ready = t + SYNC
                    if u.nrem == 0:
                        heapq.heappush(readyq[u.eng], (u.seq, id(u), u))
                        try_start(u.eng, t)
        assert nsched == len(allops), (nsched, len(allops))
        self.sim_time = max(free_at.values())
        for e in ENGS:
            self.ops[e] = order[e]
            nd = 0
            for i, o in enumerate(order[e]):
                o.idx = i
                if o.is_dma:
                    o.dslot = nd % NDS
                    o.dval = 16 * (nd // NDS + 1)
                    nd += 1

    def emit(self, final_bufs=()):
        nc = self.nc
        self.op("sp", None, reads=list(final_bufs), writes=(), cost=0.01)
        if SCHEDULE:
            self.schedule()
        else:
            for e in ENGS:
                nd = 0
                for o in self.ops[e]:
                    if o.is_dma:
                        o.dslot = nd % NDS
                        o.dval = 16 * (nd // NDS + 1)
                        nd += 1
        for e in ENGS:
            for o in self.ops[e]:
                for d in o.deps:
                    if not d.is_dma:
                        if d.eng != o.eng or SAME_ENGINE_SYNC[o.eng] or o.is_dma:
                            d.ref = True
        nsem_eng = {}
        for e in ENGS:
            c = 0
            for o in self.ops[e]:
                if o.ref and not o.is_dma:
                    c += 1
                    o.semval = c
            nsem_eng[e] = c // EPOCH + 1
        sems = {}
        for e in ENGS:
            sems[e] = [self.stack.enter_context(nc.semaphore(f"s_{e}_{i}")) for i in range(nsem_eng[e])]
        dsems = {}
        for e in ENGS:
            if self.ndma[e]:
                dsems[e] = [self.stack.enter_context(nc.semaphore(f"d_{e}_{i}")) for i in range(NDS)]

        def replay(ename, eh):
            known = {}
            for o in self.ops[ename]:
                waits = {}
                for d in o.deps:
                    if d.is_dma:
                        key = ("d", d.eng, d.dslot)
                        val = d.dval
                    else:
                        if d.eng == ename and not (SAME_ENGINE_SYNC[ename] or o.is_dma):
                            continue
                        key = ("e", d.eng)
                        val = d.semval
                    if known.get(key, 0) >= val:
                        continue
                    if waits.get(key, 0) < val:
                        waits[key] = val
                if o.is_dma:
                    key = ("d", ename, o.dslot)
                    val = o.dval - 16
                    if val > 0 and known.get(key, 0) < val and waits.get(key, 0) < val:
                        waits[key] = val
                for key, val in waits.items():
                    known[key] = val
                    if key[0] == "d":
                        eh.wait_ge(dsems[key[1]][key[2]], val)
                    else:
                        v0 = val - 1
                        eh.wait_ge(sems[key[1]][v0 // EPOCH], v0 % EPOCH + 1)
                if o.fn is None:
                    continue
                inst = o.fn(eh)
                if o.is_dma:
                    inst.then_inc(dsems[ename][o.dslot], 16)
                elif o.ref:
                    v0 = o.semval - 1
                    inst.then_inc(sems[ename][v0 // EPOCH], 1)

        with nc.Block() as block:
            @block.tensor
            def _(e):
                replay("pe", e)

            @block.scalar
            def _(e):
                replay("act", e)

            @block.vector
            def _(e):
                replay("dve", e)

            @block.gpsimd
            def _(e):
                replay("pool", e)

            @block.sync
            def _(e):
                replay("sp", e)
        self.stack.close()


WS_ELEMS = 4096
NSLOT = 4
ARENA = 32000
NCONST = 1432
SWA_CFG = ((128, 1), (512, 4), (2048, 16))
IN0 = 7444


class Ctx:
    pass


def build(dbg=()):
    nc = bass.Bass("TRN2", target_bir_lowering=False)
    fw = FW(nc)
    C = Ctx()

    def din(name, shape, dt=F32):
        return nc.dram_tensor(name, list(shape), dt, kind="ExternalInput").ap()

    def dout(name, shape, dt=F32):
        return nc.dram_tensor(name, list(shape), dt, kind="ExternalOutput").ap()

    def dscr(name, shape, dt=F32):
        kind = "ExternalOutput" if name in dbg else "Internal"
        return nc.dram_tensor(name, list(shape), dt, kind=kind).ap()

    L = [LP, LS]
    CH = [64, 4]
    xin = [din("xp", [LP, D]), din("xs", [LS, D])]
    cvec = din("cvec", [2, D])
    ada_w = din("ada_w", [2, 72, 128, KD * 256])
    ada_b = din("ada_b", [2, 9 * D])
    lns = din("lns", [7, D])
    w_up = din("ffn_w_up", [2, 2, FC, 128, KD * 256])
    w_dn = din("ffn_w_down", [2, 2, 32, 128, 11 * 256])
    w_in0 = din("w_in0", [33, 128, KD * 256])
    gconvw = din("gdn_conv_w", [4, 3840])
    galog = din("gdn_a_log", [10])
    gdtb = din("gdn_dt_bias", [10])
    gonorm = din("gdn_onorm", [128])
    w_out0 = din("w_out0", [8, 128, KD * 256])
    sc_w_in = din("sc_w_in", [24, 128, KD * 256])
    sc_conv_w = din("sc_conv_w", [3, D])
    sc_w_out = din("sc_w_out", [8, 128, KD * 256])
    consts_d = din("consts", [128, NCONST])
    st_S = din("st_S", [10, 128, 128])
    st_conv = din("st_conv", [3, 3840])
    st_kv = [din(f"st_kv{g}", [SWA_CFG[g][0], 2, 4, 64]) for g in range(3)]
    st_sc = din("st_sc", [2, D])
    gidx_d = din("gidx", [128, 3], mybir.dt.uint32)
    hflag_d = din("hflag", [128, 1])
    OWN = 1024
    yout = [dout("y_p", [OWN, D]), dout("y_s", [LS, D])]
    o_S = [dout("o_S_p", [10, 128, 128]), dout("o_S_s", [10, 128, 128])]
    o_conv = [dout("o_conv_p", [3, 3840]), dout("o_conv_s", [3, 3840])]
    o_kv = [[dout(f"o_kv{g}_p", [SWA_CFG[g][0], 2, 4, 64]) for g in range(3)],
            [dout(f"o_kv{g}_s", [SWA_CFG[g][0], 2, 4, 64]) for g in range(3)]]
    o_sc = [dout("o_sc_p", [2, D]), dout("o_sc_s", [2, D])]
    X1T = [dscr("x1T_p", [D, 4]), dscr("x1T_s", [D, LS])]
    NTL = LP // TT
    X1B = dscr("x1B", [NTL * 128, KD * TT])
    X1H = dscr("x1H", [NTL * 128, KD * 2])
    MIXB = dscr("mixB", [NTL * 128, KD * TT], BF16)
    MIXH = dscr("mixH", [NTL * 128, KD * 2], BF16)
    b_X1B, b_MIXB = fw.buf("x1b"), fw.buf("mixb")
    PJT = [dscr("pjT_p", [6656, LP]), dscr("pjT_s", [6656, LS])]
    ABTM = [dscr("ab_p", [LP, 20]), dscr("ab_s", [LS, 20])]
    KVTM = [dscr("kv_p", [LP, 1536]), dscr("kv_s", [LS, 1536])]
    MIXT = [dscr("mixT_p", [D, LP], BF16), dscr("mixT_s", [D, LS], BF16)]
    OB = [dscr("ob_p", [LP, 12, 65]), dscr("ob_s", [LS, 12, 65])]

    arena = fw.sbuf("arena", [128, ARENA], F32)
    wsl = [fw.sbuf(f"wsl{i}", [128, WS_ELEMS], BF16) for i in range(NSLOT)]
    wsl_b = [fw.buf(f"wsl{i}") for i in range(NSLOT)]
    xtm = [fw.sbuf(f"xtm{i}", [128, D], F32) for i in range(2)]
    xtm_b = [fw.buf() for _ in range(2)]
    consts = fw.sbuf("consts_sb", [128, NCONST], F32)
    ident = consts[:, 0:128]
    triU = consts[:, 128:256]
    triUs = consts[:, 256:384]
    triL = consts[:, 384:512]
    triLs = consts[:, 512:640]
    smask = consts[:, 640:664].rearrange("p (a b) -> p a b", b=4)
    lmask = consts[:, 664:1432].rearrange("p (a b) -> p a b", b=64)
    ones = fw.sbuf("ones", [128, 128], F32)
    modT = fw.sbuf("modT", [128, 2, 144, 2], F32)
    Amod = fw.sbuf("Amod", [128, 2, 3, KD, 2], F32)
    Gmod = fw.sbuf("Gmod", [128, 2, 3, KD, 2], F32)
    lnT = fw.sbuf("lnT", [128, 7, KD], F32)
    adabT = fw.sbuf("adabT", [128, 2, 144], F32)
    cT = fw.sbuf("cT", [128, KD, 2], F32)
    csT = fw.sbuf("csT", [128, KD, 2], BF16)
    scw = fw.sbuf("scw", [128, 3, KD], F32)
    gcw = fw.sbuf("gcw", [128, 4, 30], F32)
    alb = fw.sbuf("alb", [128, 10], F32)
    dtb = fw.sbuf("dtb", [128, 10], F32)
    onrm = fw.sbuf("onrm", [128, 1], F32)
    epsT = fw.sbuf("epsT", [128, 1], F32)
    fsc = fw.sbuf("fsc", [128, 8], F32)
    gidx = fw.sbuf("gidx_sb", [128, 3], mybir.dt.uint32)
    hflag = fw.sbuf("hflag_sb", [128, 1], F32)
    hst = fw.sbuf("hst", [128, KD * 2], F32)
    hstb = fw.sbuf("hstb", [128, KD * 2], BF16)
    halo = fw.sbuf("halo", [128, KD, 2], F32)
    b_hst, b_halo = fw.buf("hst"), fw.buf("halo")
    sq = [fw.sbuf(f"sq{i}", [128, TT], F32) for i in range(2)]
    sq_b = [fw.buf() for _ in range(2)]
    rstd = fw.sbuf("rstd", [128, TT], F32)
    tmpn = [fw.sbuf(f"tmpn{i}", [128, TT], F32) for i in range(2)]
    tmpn_b = [fw.buf() for _ in range(2)]
    sg = [fw.sbuf(f"sg{i}", [128, TT], F32) for i in range(2)]
    sg_b = [fw.buf() for _ in range(2)]
    b_const, b_mod, b_rstd, b_cs = (fw.buf(n) for n in ("const", "mod", "rstd", "cs"))
    b_X1T = [fw.buf(), fw.buf()]
    b_PJT = [fw.buf(), fw.buf()]
    b_AB = [fw.buf(), fw.buf()]
    b_KV = [fw.buf(), fw.buf()]
    b_MIX = [fw.buf(), fw.buf()]
    b_OB = [fw.buf(), fw.buf()]
    b_out = fw.buf("out")
    psall = fw.psum("psall", [128, 8, 512])
    ps = [psall[:, i, :] for i in range(8)]
    ps_b = [fw.buf(f"ps{i}") for i in range(8)]
    C.lin_rr = 0
    C.slot_rr = 0
    C.misc_rr = 0
    C.pb_rr = 0
    C.aoff = 0

    def misc_bank():
        i = 6 + (C.misc_rr % 2)
        C.misc_rr += 1
        return ps[i], ps_b[i]

    def pbank():
        i = C.pb_rr % 8
        C.pb_rr += 1
        return ps[i], ps_b[i]

    def carve(shape, dt):
        n = 1
        for x in shape[1:]:
            n *= x
        nbytes = n * (4 if dt == F32 else 2)
        nw = (nbytes + 31) // 32 * 8
        assert C.aoff + nw <= ARENA, ("arena overflow", C.aoff, nw)
        v = arena[0:shape[0], C.aoff:C.aoff + nw]
        C.aoff += nw
        if dt != F32:
            v = v.bitcast(dt)
        v = v[:, 0:n]
        if len(shape) == 3:
            v = v.rearrange("p (a b) -> p a b", b=shape[2])
        elif len(shape) == 4:
            v = v.rearrange("p (a b c) -> p a b c", b=shape[2], c=shape[3])
        return v, fw.buf()

    def stage_reset():
        fw.fence(fsc[:, 0:1])
        C.aoff = 0

    def fsz(ap):
        n = 1
        for x in ap.shape[1:]:
            n *= x
        return n

    def MM(o, l, r, f, la, reads, writes):
        c = (max(64, fsz(r)) / 2400.0) * (4.0 if r.dtype == F32 else 1.0) + 0.02
        fw.pe(lambda e: e.matmul(o, l, r, start=f, stop=la), reads, writes, cost=c)

    def TR(o, i, idn, reads, writes):
        fw.pe(lambda e: e.transpose(o, i, idn), reads, writes, cost=0.15)

    def ACT(out, in_, func, reads, writes, scale=None, bias=None):
        kw = {}
        if scale is not None:
            kw["scale"] = scale
        if bias is not None:
            kw["bias"] = bias
        fw.act(lambda e: e.activation(out=out, in_=in_, func=func, **kw), reads, writes, cost=0.2 + fsz(out) / 1200.0)

    def TT_(out, in0, in1, op, reads, writes):
        fw.dve(lambda e: e.tensor_tensor(out=out, in0=in0, in1=in1, op=op), reads, writes, cost=0.12 + fsz(out) / 960.0)

    def TS(out, in0, s1, s2, op0, op1, reads, writes):
        if op1 is None:
            fw.dve(lambda e: e.tensor_scalar(out=out, in0=in0, scalar1=s1, scalar2=None, op0=op0), reads, writes,
                   cost=0.12 + fsz(out) / 960.0)
        else:
            fw.dve(lambda e: e.tensor_scalar(out=out, in0=in0, scalar1=s1, scalar2=s2, op0=op0, op1=op1), reads, writes,
                   cost=0.12 + fsz(out) / 960.0)

    def STT(out, in0, scalar, in1, op0, op1, reads, writes):
        fw.dve(lambda e: e.scalar_tensor_tensor(out=out, in0=in0, scalar=scalar, in1=in1, op0=op0, op1=op1), reads, writes,
               cost=0.12 + fsz(out) / 960.0)

    def CP(out, in_, reads, writes):
        fw.dve(lambda e: e.tensor_copy(out=out, in_=in_), reads, writes, cost=0.12 + fsz(out) / 960.0)

    def ACP(out, in_, reads, writes):
        fw.act(lambda e: e.copy(out=out, in_=in_), reads, writes, cost=0.2 + fsz(out) / 1200.0)

    def MEMSET(ap, val, writes):
        fw.dve(lambda e: e.memset(ap, val), (), writes)

    def RECIP(out, in_, reads, writes):
        fw.dve(lambda e: e.reciprocal(out=out, in_=in_), reads, writes, cost=0.12 + fsz(out) / 960.0)

    def DMA(q, out, in_, reads, writes, slow=False):
        if slow:
            fw.dma(q, out, in_, reads, writes, allow_slow_non_contiguous=True)
        else:
            fw.dma(q, out, in_, reads, writes)

    def bc(ap, shape):
        return ap.to_broadcast(shape)

    DMA("sp", consts[:], consts_d[:, :], [], [b_const])
    DMA("sp", gidx[:], gidx_d[:, :], [], [b_const])
    DMA("sp", hflag[:], hflag_d[:, :], [], [b_const])

    def GATHER(out2d, src2d, col, reads, writes):
        fw.op("pool", lambda e: e.indirect_dma_start(out=out2d, out_offset=None, in_=src2d,
                                                     in_offset=bass.IndirectOffsetOnAxis(ap=gidx[:, col:col + 1], axis=0)),
              list(reads) + [b_const], writes, dma=True)
    MEMSET(ones[:], 1.0, [b_const])
    MEMSET(epsT[:], EPS, [b_const])
    for j in range(7):
        DMA("sp", lnT[:, j, :], lns[j].rearrange("(k p) -> p k", p=128), [], [b_const], slow=True)
    for l in range(2):
        DMA("sp", adabT[:, l, :], ada_b[l].rearrange("(m p) -> p m", p=128), [], [b_const], slow=True)
    for s in range(2):
        DMA("sp", cT[:, :, s], cvec[s].rearrange("(k p) -> p k", p=128), [], [b_const], slow=True)
    for j in range(3):
        DMA("sp", scw[:, j, :], sc_conv_w[j].rearrange("(k p) -> p k", p=128), [], [b_const], slow=True)
    for j in range(4):
        DMA("sp", gcw[:, j, :], gconvw[j].rearrange("(k p) -> p k", p=128), [], [b_const], slow=True)
    DMA("sp", alb[:], galog.partition_broadcast(128), [], [b_const])
    DMA("sp", dtb[:], gdtb.partition_broadcast(128), [], [b_const])
    DMA("sp", onrm[:], gonorm.rearrange("(p o) -> p o", o=1), [], [b_const], slow=True)
    ACT(csT[:], cT[:], AF.Silu, [b_const], [b_cs])
    nea = fw.sbuf("nea", [128, 10], F32)
    ACT(nea[:], alb[:], AF.Exp, [b_const], [b_const])
    TS(nea[:], nea[:], -1.0, None, ALU.mult, None, [b_const], [b_const])

    def load_wtile(Wt, nkc, GC):
        si = C.slot_rr % NSLOT
        C.slot_rr += 1
        assert nkc * GC <= WS_ELEMS, (nkc, GC)
        view = wsl[si][:, 0:nkc * GC].rearrange("p (k c) -> p k c", c=GC)
        DMA("pool", wsl[si][:, 0:nkc * GC], Wt, [], [wsl_b[si]])
        return si, view

    def linear_fm(W, groups, kparts, rhs_fn, rhs_bufs, T, evac, extra=()):
        tiles = [(gi, kp) for gi in range(len(groups)) for kp in range(len(kparts))]
        loaded = {}
        PF = NSLOT - 2
        st = {"nxt": 0}

        def ensure(upto):
            while st["nxt"] < len(tiles) and st["nxt"] <= upto:
                gi, kp = tiles[st["nxt"]]
                loaded[st["nxt"]] = load_wtile(W[st["nxt"]], kparts[kp][1], sum(n for _, n in groups[gi]))
                st["nxt"] += 1

        ti = 0
        for gi, segs in enumerate(groups):
            GC = sum(n for _, n in segs)
            nch = (GC + 127) // 128
            assert nch <= 2
            base = (C.lin_rr % 3) * 2
            C.lin_rr += 1
            for kp, (k0c, nkc) in enumerate(kparts):
                ensure(ti + PF)
                si, view = loaded.pop(ti)
                ti += 1
                for ci in range(nch):
                    rows = min(128, GC - ci * 128)
                    for kl in range(nkc):
                        kc = k0c + kl
                        first = (kp == 0 and kl == 0)
                        last = (kp == len(kparts) - 1 and kl == nkc - 1)
                        MM(ps[base + ci][0:rows, 0:T], view[:, kl, ci * 128:ci * 128 + rows], rhs_fn(kc), first, last,
                           [wsl_b[si]] + list(rhs_bufs), [ps_b[base + ci]])
                        for (rf_e, rb_e, T_e, ev_e) in extra:
                            MM(ps[6 + ci][0:rows, 0:T_e], view[:, kl, ci * 128:ci * 128 + rows], rf_e(kc), first, last,
                               [wsl_b[si]] + list(rb_e), [ps_b[6 + ci]])
            for ci in range(nch):
                rows = min(128, GC - ci * 128)
                evac(gi, ci, rows, ps[base + ci][0:rows, 0:T], ps_b[base + ci])
                for (rf_e, rb_e, T_e, ev_e) in extra:
                    ev_e(gi, ci, rows, ps[6 + ci][0:rows, 0:T_e], ps_b[6 + ci])

    def linear_tm(W, groups, lhs_fn, lhs_bufs, T, evac, extra=()):
        ntb = (T + 127) // 128
        for gi, (col, n) in enumerate(groups):
            si, view = load_wtile(W[gi], KD, 256)
            for tb in range(ntb):
                nt = min(128, T - tb * 128)
                for kc in range(KD):
                    MM(ps[tb][0:nt, 0:n], lhs_fn(kc, tb, nt), view[:, kc, 0:n], kc == 0, kc == KD - 1,
                       [wsl_b[si]] + list(lhs_bufs), [ps_b[tb]])
                evac(gi, tb, nt, ps[tb][0:nt, 0:n], ps_b[tb])
            for (lf_e, lb_e, T_e, ev_e) in extra:
                for kc in range(KD):
                    MM(ps[4][0:T_e, 0:n], lf_e(kc, 0, T_e), view[:, kc, 0:n], kc == 0, kc == KD - 1,
                       [wsl_b[si]] + list(lb_e), [ps_b[4]])
                ev_e(gi, 0, T_e, ps[4][0:T_e, 0:n], ps_b[4])

    for l in range(2):
        groups = [[(m * 256, 256)] for m in range(72)]

        def evac_mod(gi, ci, rows, p_ap, p_b, l=l):
            m = gi * 2 + ci
            TS(modT[:, l, m, :], p_ap, adabT[:, l, m:m + 1], None, ALU.add, None, [p_b, b_const], [b_mod])
        linear_fm(ada_w[l], groups, [(0, KD)], lambda kc: csT[:, kc, :], [b_cs], 2, evac_mod)
    for l in range(2):
        for j in range(3):
            for s in range(2):
                STT(Amod[:, l, j, :, s], modT[:, l, (3 * j + 1) * KD:(3 * j + 2) * KD, s], 1.0, lnT[:, 3 * l + j, :],
                    ALU.add, ALU.mult, [b_mod, b_const], [b_mod])
                TS(Gmod[:, l, j, :, s], modT[:, l, (3 * j + 2) * KD:(3 * j + 3) * KD, s], (1.0 if j == 1 else 0.5), None,
                   ALU.mult, None, [b_mod], [b_mod])

    RW = 8
    r_xT = fw.sbuf("r_xT", [128, KD, RW], F32)
    r_hT = fw.sbuf("r_hT", [128, KD, RW], BF16)
    r_aT = fw.sbuf("r_aT", [128, FC, RW], BF16)
    r_cx = fw.sbuf("r_cx", [128, KD, 2 + RW], F32)
    r_sg = [fw.sbuf(f"r_sg{i}", [128, RW], F32) for i in range(2)]
    r_stg = [fw.sbuf(f"r_stg{i}", [128, RW], F32) for i in range(2)]
    r_sq = fw.sbuf("r_sq", [RW, 256], F32)
    r_yt = fw.sbuf("r_yt", [128, RW], F32)

    class TCx:
        pass

    def make_rider():
        tc = TCx()
        tc.xT, tc.hT, tc.aT, tc.cx = r_xT, r_hT, r_aT, r_cx
        tc.b_xT, tc.b_hT, tc.b_aT, tc.b_cx = fw.buf("rx"), fw.buf("rh"), fw.buf("ra"), fw.buf("rcx")
        tc.sg, tc.sg_b = r_sg, [fw.buf(), fw.buf()]
        tc.stg, tc.stg_b = r_stg, [fw.buf(), fw.buf()]
        tc.sqt, tc.sqt_b = [r_sq, r_sq], [fw.buf("rsq")] * 2
        tc.yt, tc.b_yt = r_yt, fw.buf("ryt")
        return tc

    def carve_tile():
        tc = TCx()
        tc.xT, tc.b_xT = carve([128, KD, TT], F32)
        tc.hT, tc.b_hT = carve([128, KD, TT], BF16)
        tc.aT, tc.b_aT = carve([128, FC, TT], BF16)
        tc.cx, tc.b_cx = None, None
        tc.sg, tc.sg_b = sg, sg_b
        tc.stg, tc.stg_b = [tmpn[0], tmpn[1]], tmpn_b
        tc.sqt, tc.sqt_b = sq, sq_b
        tc.yt, tc.b_yt = tmpn[0], tmpn_b[0]
        return tc

    def use(tc):
        C.xT, C.b_xT, C.hT, C.b_hT, C.aT, C.b_aT = tc.xT, tc.b_xT, tc.hT, tc.b_hT, tc.aT, tc.b_aT

    def load_x_tile(s, t0, T):
        xT, b_xT = C.xT, C.b_xT
        ntb = (T + 127) // 128
        for tb in range(ntb):
            n = min(128, T - tb * 128)
            bi = tb % 2
            DMA("sp", xtm[bi][0:n, :], xin[s][t0 + tb * 128:t0 + tb * 128 + n, :], [], [xtm_b[bi]])
            for k4 in range(4):
                pb, pbb = misc_bank()
                for kk in range(4):
                    k = k4 * 4 + kk
                    TR(pb[:, kk * 128:kk * 128 + n], xtm[bi][0:n, k * 128:(k + 1) * 128], ident[0:n, 0:n],
                       [xtm_b[bi], b_const], [pbb])
                CP(xT[:, k4 * 4:k4 * 4 + 4, tb * 128:tb * 128 + n],
                   pb.rearrange("p (k t) -> p k t", t=128)[:, :, 0:n], [pbb], [b_xT])

    def rms_stats(T, div):
        xT, b_xT = C.xT, C.b_xT
        pb, pbb = misc_bank()
        for k in range(KD):
            bi = k % 2
            ACT(sq[bi][:, 0:T], xT[:, k, 0:T], AF.Square, [b_xT], [sq_b[bi]])
            MM(pb[:, 0:T], ones[:, :], sq[bi][:, 0:T], k == 0, k == KD - 1, [sq_b[bi], b_const], [pbb])
        ACT(rstd[:, 0:T], pb[:, 0:T], AF.Sqrt, [pbb, b_const], [b_rstd], scale=1.0 / div, bias=epsT[:, 0:1])
        RECIP(rstd[:, 0:T], rstd[:, 0:T], [b_rstd], [b_rstd])

    def norm_mod(l, j, segs, T):
        xT, b_xT, hT, b_hT = C.xT, C.b_xT, C.hT, C.b_hT
        rms_stats(T, D)
        for k in range(KD):
            bi = k % 2
            TT_(tmpn[bi][:, 0:T], xT[:, k, 0:T], rstd[:, 0:T], ALU.mult, [b_xT, b_rstd], [tmpn_b[bi]])
            for (s, c0, n) in segs:
                ACT(hT[:, k, c0:c0 + n], tmpn[bi][:, c0:c0 + n], AF.Identity, [tmpn_b[bi], b_mod], [b_hT],
                    scale=Amod[:, l, j, k, s:s + 1], bias=modT[:, l, 3 * j * KD + k, s:s + 1])

    def resid_evac(l, j, segs, T):
        xT, b_xT = C.xT, C.b_xT

        def ev(gi, ci, rows, p_ap, p_b):
            d = gi * 2 + ci
            for (s, c0, n) in segs:
                STT(xT[:, d, c0:c0 + n], p_ap[:, c0:c0 + n], Gmod[:, l, j, d, s:s + 1], xT[:, d, c0:c0 + n], ALU.mult, ALU.add,
                    [p_b, b_mod, b_xT], [b_xT])
        return ev

    def ffn(l, i, j, tcs):
        groups = [[(m * 256, 256)] for m in range(FC)]

        def mk_up(tc):
            T = tc.T

            def evac_up(gi, ci, rows, p_ap, p_b):
                bi = gi % 2
                if ci == 0:
                    ACT(tc.sg[bi][:, 0:T], p_ap, AF.Silu, [p_b], [tc.sg_b[bi]])
                else:
                    TT_(tc.aT[:, gi, 0:T], p_ap, tc.sg[bi][:, 0:T], ALU.mult, [p_b, tc.sg_b[bi]], [tc.b_aT])
            return (lambda kc: tc.hT[:, kc, 0:T]), [tc.b_hT], T, evac_up
        parts = [mk_up(tc) for tc in tcs]
        linear_fm(w_up[l, i], groups, [(0, KD)], *parts[0], extra=parts[1:])
        groups = [[(m * 256, 256)] for m in range(8)]

        def mk_dn(tc):
            T = tc.T
            use(tc)
            return (lambda kc: tc.aT[:, kc, 0:T]), [tc.b_aT], T, resid_evac(l, j, tc.segs, T)
        parts = [mk_dn(tc) for tc in tcs]
        linear_fm(w_dn[l, i], groups, [(0, 11), (11, 11), (22, 11), (33, 11)], *parts[0], extra=parts[1:])

    def in_proj(tcs):
        groups = [[(m * 256, 256)] for m in range(20)] + [[(5140 + m * 256, 256)] for m in range(6)]

        def mk(tc):
            T, s, t0 = tc.T, tc.s, tc.t0

            def ev(gi, ci, rows, p_ap, p_b):
                row0 = gi * 256 + ci * 128
                bi = ci
                if bi == 0:
                    ACP(tc.stg[bi][:, 0:T], p_ap, [p_b], [tc.stg_b[bi]])
                else:
                    CP(tc.stg[bi][:, 0:T], p_ap, [p_b], [tc.stg_b[bi]])
                DMA("sp", PJT[s][row0:row0 + 128, t0:t0 + T], tc.stg[bi][:, 0:T], [tc.stg_b[bi]], [b_PJT[s]])
            return (lambda kc: tc.hT[:, kc, 0:T]), [tc.b_hT], T, ev
        parts = [mk(tc) for tc in tcs]
        linear_fm(w_in0[0:26], groups, [(0, KD)], *parts[0], extra=parts[1:])
        tgroups = [(5120, 20)] + [(5908 + m * 256, 256) for m in range(6)]

        def mkt(tc):
            T, s, t0 = tc.T, tc.s, tc.t0

            def evt(gi, tb, nt, p_ap, p_b):
                bi = tb % 2
                n = 20 if gi == 0 else 256
                CP(tc.sqt[bi][0:nt, 0:n], p_ap, [p_b], [tc.sqt_b[bi]])
                if gi == 0:
                    DMA("sp", ABTM[s][t0 + tb * 128:t0 + tb * 128 + nt, :], tc.sqt[bi][0:nt, 0:20], [tc.sqt_b[bi]], [b_AB[s]])
                else:
                    c0 = (gi - 1) * 256
                    DMA("sp", KVTM[s][t0 + tb * 128:t0 + tb * 128 + nt, c0:c0 + 256], tc.sqt[bi][0:nt, 0:256], [tc.sqt_b[bi]],
                        [b_KV[s]])
            return (lambda kc, tb, nt: tc.hT[:, kc, tb * 128:tb * 128 + nt]), [tc.b_hT], T, evt
        partst = [mkt(tc) for tc in tcs]
        linear_tm(w_in0[26:33], tgroups, *partst[0], extra=partst[1:])

    def store_xT(s, t0, T):
        if s == 1:
            DMA("sp", X1T[1].rearrange("(k p) t -> p k t", p=128)[:, :, 0:T], C.xT[:, :, 0:T], [C.b_xT], [b_X1T[1]])
        else:
            tl = t0 // TT
            DMA("sp", X1B[tl * 128:(tl + 1) * 128, :], C.xT[:].rearrange("p k t -> p (k t)"), [C.b_xT], [b_X1B])
            DMA("sp", X1H[tl * 128:(tl + 1) * 128, :].rearrange("p (k t) -> p k t", t=2), C.xT[:, :, TT - 2:TT], [C.b_xT], [b_X1B])

    ptiles = [(0, t0, TT) for t0 in range(0, LP, TT)]
    if "short" in dbg:
        ptiles = [(0, t0, TT) for t0 in range(0, 2048, TT)]
        L = [2048, LS]
    stage_reset()
    mtc = carve_tile()
    rtc = make_rider()
    for ti_, (s, t0, T) in enumerate(ptiles):
        mtc.s, mtc.t0, mtc.T, mtc.segs = s, t0, T, [(s, 0, T)]
        tcs = [mtc]
        if ti_ == 0:
            rtc.s, rtc.t0, rtc.T, rtc.segs = 1, 0, LS, [(1, 0, LS)]
            tcs.append(rtc)
        for tc in tcs:
            use(tc)
            load_x_tile(tc.s, tc.t0, tc.T)
            norm_mod(0, 0, tc.segs, tc.T)
        ffn(0, 0, 0, tcs)
        for tc in tcs:
            use(tc)
            store_xT(tc.s, tc.t0, tc.T)
            norm_mod(0, 1, tc.segs, tc.T)
        in_proj(tcs)

    if "stopA" in dbg:
        fw.emit(final_bufs=list(fw.allbufs))
        return nc

    def gdn(s):
        Cc = CH[s]
        nchunks = L[s] // Cc
        nsteps = {64: 5, 4: 1}[Cc]
        NB5 = 5 * Cc
        rawT, b_raw = carve([128, 40, 3 + Cc], F32)
        acc, b_acc = carve([128, 30, Cc], F32)
        tmp, b_tmp = carve([128, 30, Cc], F32)
        rn, b_rn = carve([128, 20, Cc], F32)
        qkb, b_qkb = carve([128, 20, Cc], BF16)
        grow, b_grow = carve([128, 10, Cc], F32)
        brow, b_brow = carve([128, 10, Cc], F32)
        egr, b_egr = carve([128, 10, Cc], F32)
        bge, b_bge = carve([128, 10, Cc], F32)
        rhsg, b_rhsg = carve([Cc, 10, Cc], F32)
        rhsb, b_rhsb = carve([Cc, 10, Cc], F32)
        dd, b_dd = carve([Cc, 10, Cc], F32)
        decT, b_decT = carve([Cc, 10, Cc], F32)
        decN, b_decN = carve([Cc, 10, Cc], F32)
        t1, b_t1 = carve([Cc, 10, Cc], F32)
        t2, b_t2 = carve([Cc, 10, Cc], F32)
        t3, b_t3 = carve([Cc, 10, Cc], F32)
        Nk = [carve([Cc, 10, Cc], BF16) for _ in range(2)]
        Bk = [carve([Cc, 10, Cc], BF16) for _ in range(2)]
        Wk = [carve([Cc, 10, Cc], BF16) for _ in range(2)]
        Wtk = [carve([Cc, 10, Cc], BF16) for _ in range(2)]
        Boff, b_Boff = carve([Cc, 10, Cc], BF16)
        Noff, b_Noff = carve([Cc, 10, Cc], BF16)
        Xs, b_Xs = carve([Cc, 10, Cc], BF16)
        X2s, b_X2s = carve([Cc, 10, Cc], BF16)
        QKT, b_QKT = carve([Cc, 10, Cc], BF16)
        qdT, b_qdT = carve([128, 10, Cc], BF16)
        kbgT, b_kbgT = carve([128, 10, Cc], BF16)
        kd, b_kd = carve([Cc, 10, 128], BF16)
        vb, b_vb = carve([Cc, 10, 128], F32)
        rb, b_rb = carve([Cc, 10, 128], BF16)
        ub, b_ub = carve([Cc, 10, 128], BF16)
        S, b_S = carve([128, 10, 128], F32)
        Sb, b_Sb = carve([128, 10, 128], BF16)
        oT, b_oT = carve([128, 10, Cc], F32)
        o2, b_o2 = carve([128, 10, Cc], F32)
        sz, b_sz = carve([128, 10, Cc], F32)
        mixa, b_mixa = carve([128, 10, Cc], BF16)
        abt, b_abt = carve([Cc, 20], F32)
        sm, b_sm = carve([Cc, 8, 10], F32)

        if s == 0:
            MEMSET(S[:], 0.0, [b_S])
            MEMSET(rawT[:, :, 0:3], 0.0, [b_raw])
        else:
            DMA("sp", S[:], st_S.rearrange("h k v -> k h v"), [], [b_S])
            MEMSET(rawT[:, :, 0:3], 0.0, [b_raw])
            for j in range(3):
                DMA("sp", rawT[:, 0:30, j], st_conv[j].rearrange("(c p) -> p c", p=128), [], [b_raw], slow=True)
        CP(Sb[:], S[:], [b_S], [b_Sb])

        def heads_banks():
            (pa, pab), (pb_, pbb) = pbank(), pbank()
            return [(pa, pab), (pb_, pbb)]

        for ci in range(nchunks):
            c0 = ci * Cc
            DMA("sp", rawT[:, :, 3:3 + Cc], PJT[s][0:5120, c0:c0 + Cc].rearrange("(c p) t -> p c t", p=128),
                [b_PJT[s]], [b_raw])
            DMA("sp", abt[:], ABTM[s][c0:c0 + Cc, :], [b_AB[s]], [b_abt])
            for j in range(4):
                w_bc = gcw[:, j, :].unsqueeze(2).to_broadcast([128, 30, Cc])
                if j == 0:
                    TT_(acc[:], rawT[:, 0:30, 0:Cc], w_bc, ALU.mult, [b_raw, b_const], [b_acc])
                else:
                    TT_(tmp[:], rawT[:, 0:30, j:j + Cc], w_bc, ALU.mult, [b_raw, b_const], [b_tmp])
                    TT_(acc[:], acc[:], tmp[:], ALU.add, [b_acc, b_tmp], [b_acc])
            ACT(acc[:], acc[:], AF.Silu, [b_acc], [b_acc])
            ACT(sz[:], rawT[:, 30:40, 3:3 + Cc], AF.Silu, [b_raw], [b_sz])
            if ci == nchunks - 1:
                for j in range(3):
                    DMA("sp", o_conv[s][j].rearrange("(c p) -> p c", p=128), rawT[:, 0:30, Cc + j], [b_raw], [b_out], slow=True)
            else:
                CP(tmp[:, :, 0:3], rawT[:, 0:30, Cc:Cc + 3], [b_raw], [b_tmp])
                CP(rawT[:, 0:30, 0:3], tmp[:, :, 0:3], [b_tmp], [b_raw])
            ACT(tmp[:, 0:20, :], acc[:, 0:20, :], AF.Square, [b_acc], [b_tmp])
            tmpf = tmp[:, 0:20, :].rearrange("p a b -> p (a b)")
            rnf = rn[:].rearrange("p a b -> p (a b)")
            ncol = 20 * Cc
            for o in range(0, ncol, 512):
                n = min(512, ncol - o)
                pb, pbb = pbank()
                MM(pb[:, 0:n], ones[:, :], tmpf[:, o:o + n], True, True, [b_tmp, b_const], [pbb])
                ACT(rnf[:, o:o + n], pb[:, 0:n], AF.Sqrt, [pbb, b_const], [b_rn], bias=epsT[:, 0:1])
            RECIP(rn[:], rn[:], [b_rn], [b_rn])
            STT(acc[:, 0:10, :], acc[:, 0:10, :], 128.0 ** -0.5, rn[:, 0:10, :], ALU.mult, ALU.mult, [b_acc, b_rn], [b_acc])
            TT_(acc[:, 10:20, :], acc[:, 10:20, :], rn[:, 10:20, :], ALU.mult, [b_acc, b_rn], [b_acc])
            ACP(qkb[:], acc[:, 0:20, :], [b_acc], [b_qkb])
            xa, t_e, t_r, gg, bt, gcol, ekd = (sm[:, i, :] for i in range(7))
            TT_(xa, abt[:, 0:10], dtb[0:Cc, :], ALU.add, [b_abt, b_const], [b_sm])
            ACT(t_e, xa, AF.Abs, [b_sm], [b_sm])
            ACT(t_e, t_e, AF.Exp, [b_sm], [b_sm], scale=-1.0)
            ACT(t_e, t_e, AF.Ln, [b_sm, b_const], [b_sm], bias=ones[0:Cc, 0:1])
            TS(t_r, xa, 0.0, None, ALU.max, None, [b_sm], [b_sm])
            TT_(t_r, t_r, t_e, ALU.add, [b_sm], [b_sm])
            TT_(gg, t_r, nea[0:Cc, :], ALU.mult, [b_sm, b_const], [b_sm])
            ACT(bt, abt[:, 10:20], AF.Sigmoid, [b_abt], [b_sm])
            TT_(rhsg[:], triU[0:Cc, 0:Cc].unsqueeze(1).to_broadcast([Cc, 10, Cc]), gg.unsqueeze(2).to_broadcast([Cc, 10, Cc]),
                ALU.mult, [b_const, b_sm], [b_rhsg])
            TT_(rhsb[:], ident[0:Cc, 0:Cc].unsqueeze(1).to_broadcast([Cc, 10, Cc]), bt.unsqueeze(2).to_broadcast([Cc, 10, Cc]),
                ALU.mult, [b_const, b_sm], [b_rhsb])
            pb, pbb = pbank()
            MM(pb[0:Cc, 0:10], triU[0:Cc, 0:Cc], gg, True, True, [b_const, b_sm], [pbb])
            CP(gcol, pb[0:Cc, 0:10], [pbb], [b_sm])
            for (src, sb_, dst, db_) in ((rhsg, b_rhsg, grow, b_grow), (rhsb, b_rhsb, brow, b_brow)):
                for hb in range(2):
                    pb, pbb = pbank()
                    MM(pb[:, 0:NB5], ones[0:Cc, :], src[:, hb * 5:hb * 5 + 5, :].rearrange("p a b -> p (a b)"), True, True,
                       [sb_, b_const], [pbb])
                    ACP(dst[:, hb * 5:hb * 5 + 5, :].rearrange("p a b -> p (a b)"), pb[:, 0:NB5], [pbb], [db_])
            TT_(dd[:], grow[0:Cc, :, :], gcol.unsqueeze(2).to_broadcast([Cc, 10, Cc]), ALU.subtract, [b_grow, b_sm], [b_dd])
            TS(decT[:], dd[:], 0.0, None, ALU.min, None, [b_dd], [b_decT])
            ACT(decT[:], decT[:], AF.Exp, [b_decT], [b_decT])
            TS(decN[:], dd[:], 0.0, -1.0, ALU.max, ALU.mult, [b_dd], [b_decN])
            ACT(decN[:], decN[:], AF.Exp, [b_decN], [b_decN])
            kkb = heads_banks()
            for h in range(10):
                pb, pbb = kkb[h // 5]
                o = (h % 5) * Cc
                MM(pb[0:Cc, o:o + Cc], qkb[:, 10 + h, :], qkb[:, 10 + h, :], True, True, [b_qkb], [pbb])
            qkbk = heads_banks()
            for h in range(10):
                pb, pbb = qkbk[h // 5]
                o = (h % 5) * Cc
                MM(pb[0:Cc, o:o + Cc], qkb[:, 10 + h, :], qkb[:, h, :], True, True, [b_qkb], [pbb])
            mU = triU[0:Cc, 0:Cc].unsqueeze(1).to_broadcast([Cc, 10, Cc])
            mUs = triUs[0:Cc, 0:Cc].unsqueeze(1).to_broadcast([Cc, 10, Cc])
            mLs = triLs[0:Cc, 0:Cc].unsqueeze(1).to_broadcast([Cc, 10, Cc])
            idb = ident[0:Cc, 0:Cc].unsqueeze(1).to_broadcast([Cc, 10, Cc])
            TT_(t1[:], decT[:], mUs, ALU.mult, [b_decT, b_const], [b_t1])
            TT_(t1[:], t1[:], brow[0:Cc, :, :], ALU.mult, [b_t1, b_brow], [b_t1])
            TT_(t2[:], decN[:], mLs, ALU.mult, [b_decN, b_const], [b_t2])
            TT_(t2[:], t2[:], bt.unsqueeze(2).to_broadcast([Cc, 10, Cc]), ALU.mult, [b_t2, b_sm], [b_t2])
            TT_(t3[:], decT[:], mU, ALU.mult, [b_decT, b_const], [b_t3])
            (N0, b_N0), (B0, b_B0) = Nk[0], Bk[0]
            for hb in range(2):
                pb, pbb = kkb[hb]
                kv_ = pb[0:Cc, 0:NB5].rearrange("p (a b) -> p a b", b=Cc)
                STT(N0[:, hb * 5:hb * 5 + 5, :], kv_, -1.0, t1[:, hb * 5:hb * 5 + 5, :], ALU.mult, ALU.mult, [pbb, b_t1], [b_N0])
                STT(B0[:, hb * 5:hb * 5 + 5, :], kv_, -1.0, t2[:, hb * 5:hb * 5 + 5, :], ALU.mult, ALU.mult, [pbb, b_t2], [b_B0])
                pb, pbb = qkbk[hb]
                qv_ = pb[0:Cc, 0:NB5].rearrange("p (a b) -> p a b", b=Cc)
                TT_(QKT[:, hb * 5:hb * 5 + 5, :], qv_, t3[:, hb * 5:hb * 5 + 5, :], ALU.mult, [pbb, b_t3], [b_QKT])
            levels = [bsz for bsz in (1, 2, 4, 8, 16, 32) if bsz < Cc]
            (Wc, b_Wc), (Wtc, b_Wtc) = Wk[0], Wtk[0]
            CP(Wc[:], idb, [b_const], [b_Wc])
            CP(Wtc[:], idb, [b_const], [b_Wtc])
            cur = 0
            for li, bsz in enumerate(levels):
                lidx = (1, 2, 4, 8, 16, 32).index(bsz)
                (Wc, b_Wc), (Wtc, b_Wtc) = Wk[cur], Wtk[cur]
                (Wn, b_Wn), (Wtn, b_Wtn) = Wk[1 - cur], Wtk[1 - cur]
                last = li == len(levels) - 1
                mN = lmask[0:Cc, lidx, 0:Cc].unsqueeze(1).to_broadcast([Cc, 10, Cc])
                mT = lmask[0:Cc, 6 + lidx, 0:Cc].unsqueeze(1).to_broadcast([Cc, 10, Cc])
                TT_(Boff[:], B0[:], mN, ALU.mult, [b_B0, b_const], [b_Boff])
                TT_(Noff[:], N0[:], mT, ALU.mult, [b_N0, b_const], [b_Noff])
                if not last:
                    xb = heads_banks()
                    for h in range(10):
                        pb, pbb = xb[h // 5]
                        o = (h % 5) * Cc
                        MM(pb[0:Cc, o:o + Cc], Noff[:, h, :], Wc[:, h, :], True, True, [b_Noff, b_Wc], [pbb])
                    for hb in range(2):
                        pb, pbb = xb[hb]
                        ACP(Xs[:, hb * 5:hb * 5 + 5, :].rearrange("p a b -> p (a b)"), pb[0:Cc, 0:NB5], [pbb], [b_Xs])
                    yb = heads_banks()
                    for h in range(10):
                        pb, pbb = yb[h // 5]
                        o = (h % 5) * Cc
                        MM(pb[0:Cc, o:o + Cc], Wtc[:, h, :], Xs[:, h, :], True, True, [b_Wtc, b_Xs], [pbb])
                    for hb in range(2):
                        pb, pbb = yb[hb]
                        TT_(Wn[:, hb * 5:hb * 5 + 5, :].rearrange("p a b -> p (a b)"), pb[0:Cc, 0:NB5],
                            Wc[:, hb * 5:hb * 5 + 5, :].rearrange("p a b -> p (a b)"), ALU.add, [pbb, b_Wc], [b_Wn])
                xb = heads_banks()
                for h in range(10):
                    pb, pbb = xb[h // 5]
                    o = (h % 5) * Cc
                    MM(pb[0:Cc, o:o + Cc], Boff[:, h, :], Wtc[:, h, :], True, True, [b_Boff, b_Wtc], [pbb])
                for hb in range(2):
                    pb, pbb = xb[hb]
                    ACP(X2s[:, hb * 5:hb * 5 + 5, :].rearrange("p a b -> p (a b)"), pb[0:Cc, 0:NB5], [pbb], [b_X2s])
                yb = heads_banks()
                for h in range(10):
                    pb, pbb = yb[h // 5]
                    o = (h % 5) * Cc
                    MM(pb[0:Cc, o:o + Cc], Wc[:, h, :], X2s[:, h, :], True, True, [b_Wc, b_X2s], [pbb])
                for hb in range(2):
                    pb, pbb = yb[hb]
                    TT_(Wtn[:, hb * 5:hb * 5 + 5, :].rearrange("p a b -> p (a b)"), pb[0:Cc, 0:NB5],
                        Wtc[:, hb * 5:hb * 5 + 5, :].rearrange("p a b -> p (a b)"), ALU.add, [pbb, b_Wtc], [b_Wtn])
                cur = 1 - cur
            Pf, b_Pf = Wtk[cur]
            ACT(egr[:], grow[:], AF.Exp, [b_grow], [b_egr])
            TT_(qdT[:], acc[:, 0:10, :], egr[:], ALU.mult, [b_acc, b_egr], [b_qdT])
            TT_(bge[:], brow[:], egr[:], ALU.mult, [b_brow, b_egr], [b_bge])
            TT_(kbgT[:], acc[:, 10:20, :], bge[:], ALU.mult, [b_acc, b_bge], [b_kbgT])
            TT_(ekd, grow[0:Cc, :, Cc - 1], gcol, ALU.subtract, [b_grow, b_sm], [b_sm])
            ACT(ekd, ekd, AF.Exp, [b_sm], [b_sm])
            for (srcoff, scal, dst, db_) in ((10, ekd, kd, b_kd), (20, bt, vb, b_vb)):
                for h0 in range(0, 10, 4):
                    nh = min(4, 10 - h0)
                    pb, pbb = pbank()
                    for hh in range(nh):
                        TR(pb[0:Cc, hh * 128:(hh + 1) * 128], acc[:, srcoff + h0 + hh, :], ident[:, :], [b_acc, b_const], [pbb])
                    TT_(dst[:, h0:h0 + nh, :], pb[0:Cc, 0:nh * 128].rearrange("p (a b) -> p a b", b=128),
                        scal[:, h0:h0 + nh].unsqueeze(2).to_broadcast([Cc, nh, 128]), ALU.mult, [pbb, b_sm], [db_])
            for h0 in range(0, 10, 4):
                nh = min(4, 10 - h0)
                pb, pbb = pbank()
                for hh in range(nh):
                    MM(pb[0:Cc, hh * 128:(hh + 1) * 128], kbgT[:, h0 + hh, :], Sb[:, h0 + hh, :], True, True, [b_kbgT, b_Sb], [pbb])
                TT_(rb[:, h0:h0 + nh, :], vb[:, h0:h0 + nh, :], pb[0:Cc, 0:nh * 128].rearrange("p (a b) -> p a b", b=128),
                    ALU.subtract, [pbb, b_vb], [b_rb])
            for h0 in range(0, 10, 4):
                nh = min(4, 10 - h0)
                pb, pbb = pbank()
                for hh in range(nh):
                    MM(pb[0:Cc, hh * 128:(hh + 1) * 128], Pf[:, h0 + hh, :], rb[:, h0 + hh, :], True, True, [b_Pf, b_rb], [pbb])
                ACP(ub[:, h0:h0 + nh, :].rearrange("p a b -> p (a b)"), pb[0:Cc, 0:nh * 128], [pbb], [b_ub])
            ob_ = heads_banks()
            for h in range(10):
                pb, pbb = ob_[h // 5]
                o = (h % 5) * Cc
                MM(pb[:, o:o + Cc], Sb[:, h, :], qdT[:, h, :], True, False, [b_Sb, b_qdT], [pbb])
                MM(pb[:, o:o + Cc], ub[:, h, :], QKT[:, h, :], False, True, [b_ub, b_QKT], [pbb])
            for hb in range(2):
                pb, pbb = ob_[hb]
                ACP(oT[:, hb * 5:hb * 5 + 5, :].rearrange("p a b -> p (a b)"), pb[:, 0:NB5], [pbb], [b_oT])
            TT_(S[:], S[:], egr[:, :, Cc - 1].unsqueeze(2).to_broadcast([128, 10, 128]), ALU.mult, [b_S, b_egr], [b_S])
            for h0 in range(0, 10, 4):
                nh = min(4, 10 - h0)
                pb, pbb = pbank()
                for hh in range(nh):
                    MM(pb[:, hh * 128:(hh + 1) * 128], kd[:, h0 + hh, :], ub[:, h0 + hh, :], True, True, [b_kd, b_ub], [pbb])
                TT_(S[:, h0:h0 + nh, :], S[:, h0:h0 + nh, :], pb[:, 0:nh * 128].rearrange("p (a b) -> p a b", b=128),
                    ALU.add, [pbb, b_S], [b_S])
            ACP(Sb[:], S[:], [b_S], [b_Sb])
            ACT(o2[:], oT[:], AF.Square, [b_oT], [b_o2])
            for hb in range(2):
                pb, pbb = pbank()
                MM(pb[:, 0:NB5], ones[:, :], o2[:, hb * 5:hb * 5 + 5, :].rearrange("p a b -> p (a b)"), True, True,
                   [b_o2, b_const], [pbb])
                ACT(bge[:, hb * 5:hb * 5 + 5, :].rearrange("p a b -> p (a b)"), pb[:, 0:NB5], AF.Sqrt, [pbb, b_const], [b_bge],
                    scale=1.0 / 128, bias=epsT[:, 0:1])
            RECIP(bge[:], bge[:], [b_bge], [b_bge])
            TT_(o2[:], oT[:], bge[:], ALU.mult, [b_oT, b_bge], [b_o2])
            STT(mixa[:], o2[:], onrm[:, 0:1], sz[:], ALU.mult, ALU.mult, [b_o2, b_sz, b_const], [b_mixa])
            DMA("sp", MIXT[s][0:1280, c0:c0 + Cc].rearrange("(h p) t -> p h t", p=128), mixa[:], [b_mixa], [b_MIX[s]])
        DMA("sp", o_S[s].rearrange("h k v -> k h v"), S[:], [b_S], [b_out])

    def kv_outputs(s):
        for g, (win, dil) in enumerate(SWA_CFG):
            for t in range(2):
                src_new = KVTM[s][:, t * 768 + g * 256:t * 768 + (g + 1) * 256].rearrange("l (h c) -> l h c", c=64)
                if s == 0:
                    DMA("sp", o_kv[s][g][:, t, :, :], src_new[LP - win:LP], [b_KV[s]], [b_out])
                else:
                    DMA("sp", o_kv[s][g][0:win - LS, t, :, :], st_kv[g][LS:win, t, :, :], [], [b_out])
                    DMA("sp", o_kv[s][g][win - LS:win, t, :, :], src_new[0:LS], [b_KV[s]], [b_out])

    def swa_prompt():
        s = 0
        Lq = L[0]
        qf, b_qf = carve([64, Lq], F32)
        kf, b_kf = carve([64, Lq], F32)
        qb_, b_qb = carve([64, Lq], BF16)
        kb_, b_kb = carve([64, Lq], BF16)
        NBT = Lq // 128
        vf, b_vf = carve([128, NBT, 64], F32)
        vaug, b_vaug = carve([128, NBT, 65], BF16)
        Pm = [carve([128, 2, 128], BF16) for _ in range(2)]
        msk, b_msk = carve([128, 2, 128], F32)
        oacc, b_oacc = carve([128, NBT, 65], F32)
        CP(msk[:, 0, :], triU, [b_const], [b_msk])
        CP(msk[:, 1, :], triL, [b_const], [b_msk])
        MEMSET(vaug[:, :, 64:65], 1.0, [b_vaug])
        it = 0
        for g, (win, d) in enumerate(SWA_CFG):
            n = Lq // d
            NB = n // 128
            if NB == 0:
                continue
            for hh in range(4):
                hb = g * 4 + hh
                DMA("sp", qf[:], PJT[s][5120 + hb * 64:5120 + hb * 64 + 64, 0:Lq], [b_PJT[s]], [b_qf])
                DMA("sp", kf[:], PJT[s][5888 + hb * 64:5888 + hb * 64 + 64, 0:Lq], [b_PJT[s]], [b_kf])
                TS(qb_[:].rearrange("p (r u) -> p r u", r=d), qf[:].rearrange("p (u r) -> p r u", r=d), 0.125, None,
                   ALU.mult, None, [b_qf], [b_qb])
                ACP(kb_[:].rearrange("p (r u) -> p r u", r=d), kf[:].rearrange("p (u r) -> p r u", r=d), [b_kf], [b_kb])
                vsrc = KVTM[s][0:Lq, 768 + hb * 64:768 + hb * 64 + 64].rearrange("(b j r) c -> j r b c", j=128, r=d)
                for r in range(d):
                    DMA("sp", vf[:, r * NB:(r + 1) * NB, :], vsrc[:, r, :, :], [b_KV[s]], [b_vf])
                CP(vaug[:, :, 0:64], vf[:], [b_vf], [b_vaug])
                for r in range(d):
                    for b in range(NB):
                        blk = r * NB + b
                        col = r * n + b * 128
                        P_, b_P = Pm[it % 2]
                        it += 1
                        pb, pbb = pbank()
                        nk = 2 if b > 0 else 1
                        MM(pb[:, 0:128], kb_[:, col:col + 128], qb_[:, col:col + 128], True, True, [b_kb, b_qb], [pbb])
                        if b > 0:
                            MM(pb[:, 128:256], kb_[:, col - 128:col], qb_[:, col:col + 128], True, True, [b_kb, b_qb], [pbb])
                        ACT(P_[:, 0:nk, :].rearrange("p a b -> p (a b)"), pb[:, 0:nk * 128], AF.Exp, [pbb], [b_P])
                        TT_(P_[:, 0:nk, :], P_[:, 0:nk, :], msk[:, 0:nk, :], ALU.mult, [b_P, b_msk], [b_P])
                        po, pob = pbank()
                        MM(po[:, 0:65], P_[:, 0, :], vaug[:, blk, :], True, b == 0, [b_P, b_vaug], [pob])
                        if b > 0:
                            MM(po[:, 0:65], P_[:, 1, :], vaug[:, blk - 1, :], False, True, [b_P, b_vaug], [pob])
                        ACP(oacc[:, blk, :], po[:, 0:65], [pob], [b_oacc])
                dst = OB[s][0:Lq, hb, :].rearrange("(b j r) c -> j r b c", j=128, r=d)
                for r in range(d):
                    DMA("sp", dst[:, r, :, :], oacc[:, r * NB:(r + 1) * NB, :], [b_oacc], [b_OB[s]])

    def swa_sample():
        s = 1
        qs, b_qs = carve([64, 12, LS], F32)
        ks, b_ks = carve([64, 12, LS], F32)
        qsb, b_qsb = carve([64, 12, LS], BF16)
        ksb, b_ksb = carve([64, 12, LS], BF16)
        vn, b_vn = carve([LS, 12, 64], F32)
        vnaug, b_vnaug = carve([LS, 12, 65], BF16)
        past, b_past = carve([128, 4, 2, 4, 64], F32) if False else (None, None)
        pk, b_pk = carve([128, 4, 512], F32)
        pvaug, b_pvaug = carve([128, 4, 4, 65], BF16)
        kTb, b_kTb = carve([64, 128], BF16)
        Pp, b_Pp = carve([128, LS], BF16)
        Pn_, b_Pn = carve([LS, LS], BF16)
        osb, b_osb = carve([LS, 12, 65], F32)
        DMA("sp", qs[:], PJT[s][5120:5888, :].rearrange("(h p) t -> p h t", p=64), [b_PJT[s]], [b_qs])
        DMA("sp", ks[:], PJT[s][5888:6656, :].rearrange("(h p) t -> p h t", p=64), [b_PJT[s]], [b_ks])
        TS(qsb[:], qs[:], 0.125, None, ALU.mult, None, [b_qs], [b_qsb])
        CP(ksb[:], ks[:], [b_ks], [b_ksb])
        DMA("sp", vn[:], KVTM[s][:, 768:1536].rearrange("l (h c) -> l h c", c=64), [b_KV[s]], [b_vn])
        MEMSET(vnaug[:, :, 64:65], 1.0, [b_vnaug])
        CP(vnaug[:, :, 0:64], vn[:], [b_vn], [b_vnaug])
        MEMSET(pvaug[:, :, :, 64:65], 1.0, [b_pvaug])
        for g, (win, d) in enumerate(SWA_CFG):
            X = min(d, 4)
            src = st_kv[g].rearrange("(m x) t h c -> m x (t h c)", x=d)
            for x in range(X):
                DMA("sp", pk[:, x, :], src[:, x, :], [], [b_pk])
            for x in range(X):
                CP(pvaug[:, x, :, 0:64], pk[:, x, 256:512].rearrange("p (h c) -> p h c", c=64), [b_pk], [b_pvaug])
            for hh in range(4):
                hb = g * 4 + hh
                po, pob = ps[7], ps_b[7]
                for x in range(X):
                    pt, ptb = ps[x % 2], ps_b[x % 2]
                    TR(pt[0:64, 0:128], pk[:, x, hh * 64:(hh + 1) * 64], ident[:, :], [b_pk, b_const], [ptb])
                    CP(kTb[:], pt[0:64, 0:128], [ptb], [b_kTb])
                    pq, pqb = ps[2 + x % 2], ps_b[2 + x % 2]
                    MM(pq[:, 0:LS], kTb[:], qsb[:, hb, :], True, True, [b_kTb, b_qsb], [pqb])
                    ACT(Pp[:], pq[:, 0:LS], AF.Exp, [pqb], [b_Pp])
                    mi = 0 if g == 0 else 1 + x
                    TT_(Pp[:], Pp[:], smask[:, mi, :], ALU.mult, [b_Pp, b_const], [b_Pp])
                    MM(po[0:LS, 0:65], Pp[:], pvaug[:, x, hh, :], x == 0, False, [b_Pp, b_pvaug], [pob])
                pq, pqb = ps[4], ps_b[4]
                MM(pq[0:LS, 0:LS], ksb[:, hb, :], qsb[:, hb, :], True, True, [b_ksb, b_qsb], [pqb])
                ACT(Pn_[:], pq[0:LS, 0:LS], AF.Exp, [pqb], [b_Pn])
                nm = smask[0:LS, 5, :] if g == 0 else ident[0:LS, 0:LS]
                TT_(Pn_[:], Pn_[:], nm, ALU.mult, [b_Pn, b_const], [b_Pn])
                MM(po[0:LS, 0:65], Pn_[:], vnaug[:, hb, :], False, True, [b_Pn, b_vnaug], [pob])
                CP(osb[:, hb, :], po[0:LS, 0:65], [pob], [b_osb])
        DMA("sp", OB[s][:, :, :], osb[:], [b_osb], [b_OB[s]])

    def swa_finish(s):
        ob, b_ob = carve([128, 12, 65], F32)
        den, b_den = carve([128, 4], F32)
        obn, b_obn = carve([128, 12, 64], F32)
        obT, b_obT = carve([128, 6, 128], BF16)
        Ls = L[s]
        for t0 in range(0, Ls, 128):
            nt = min(128, Ls - t0)
            DMA("sp", ob[0:nt], OB[s][t0:t0 + nt, :, :], [b_OB[s]], [b_ob])
            TT_(den[0:nt], ob[0:nt, 0:4, 64], ob[0:nt, 4:8, 64], ALU.add, [b_ob], [b_den])
            TT_(den[0:nt], den[0:nt], ob[0:nt, 8:12, 64], ALU.add, [b_ob, b_den], [b_den])
            RECIP(den[0:nt], den[0:nt], [b_den], [b_den])
            for g in range(3):
                TT_(obn[0:nt, g * 4:(g + 1) * 4, :], ob[0:nt, g * 4:(g + 1) * 4, 0:64],
                    den[0:nt].unsqueeze(2).to_broadcast([nt, 4, 64]), ALU.mult, [b_ob, b_den], [b_obn])
            obnf = obn[:].rearrange("p a b -> p (a b)")
            for half in range(2):
                pb, pbb = pbank()
                for k in range(3):
                    kc = half * 3 + k
                    TR(pb[:, k * 128:k * 128 + nt], obnf[0:nt, kc * 128:(kc + 1) * 128], ident[0:nt, 0:nt], [b_obn, b_const], [pbb])
                CP(obT[:, half * 3:half * 3 + 3, 0:nt], pb[:, 0:384].rearrange("p (k t) -> p k t", t=128)[:, :, 0:nt], [pbb], [b_obT])
            DMA("sp", MIXT[s][1280:2048, t0:t0 + nt].rearrange("(k p) t -> p k t", p=128), obT[:, :, 0:nt], [b_obT], [b_MIX[s]])

    for s in range(2):
        stage_reset()
        gdn(s)
        if L[0] == LP:
            kv_outputs(s)
    stage_reset()
    swa_prompt()
    stage_reset()
    swa_sample()
    stage_reset()
    for s in range(2):
        swa_finish(s)

    if "stopB" in dbg:
        fw.emit(final_bufs=list(fw.allbufs))
        return nc

    stage_reset()
    mtc = carve_tile()
    mtc.cx, mtc.b_cx = carve([128, KD, 2 + TT], F32)
    rtc = make_rider()
    ntl = L[0] // TT
    for tl in range(ntl):
        hT, b_hT = mtc.hT, mtc.b_hT
        DMA("sp", hT[:], MIXT[0].rearrange("(k p) t -> p k t", p=128)[:, :, tl * TT:(tl + 1) * TT], [b_MIX[0]], [b_hT])
        DMA("sp", MIXB[tl * 128:(tl + 1) * 128, :], hT[:].rearrange("p k t -> p (k t)"), [b_hT], [b_MIXB])
        DMA("sp", MIXH[tl * 128:(tl + 1) * 128, :].rearrange("p (k t) -> p k t", t=2), hT[:, :, TT - 2:TT], [b_hT], [b_MIXB])
    nown = min(2, ntl)
    groups8 = [[(m * 256, 256)] for m in range(8)]
    for ti_ in range(nown):
        mtc.T, mtc.segs, mtc.kind = TT, [(0, 0, TT)], "own"
        tcs = [mtc]
        GATHER(mtc.xT[:].rearrange("p k t -> p (k t)"), X1B, ti_, [b_X1B], [mtc.b_xT])
        GATHER(mtc.hT[:].rearrange("p k t -> p (k t)"), MIXB, ti_, [b_MIXB], [mtc.b_hT])
        if ti_ == 0:
            rtc.T, rtc.segs, rtc.kind = LS + 2, [(1, 0, LS), (0, LS, 2)], "rider"
            tcs.append(rtc)
            DMA("sp", rtc.xT[:, :, 0:LS], X1T[1].rearrange("(k p) t -> p k t", p=128)[:, :, 0:LS], [b_X1T[1]], [rtc.b_xT])
            DMA("sp", rtc.hT[:, :, 0:LS], MIXT[1].rearrange("(k p) t -> p k t", p=128)[:, :, 0:LS], [b_MIX[1]], [rtc.b_hT])
            GATHER(hst[:, :], X1H, 2, [b_X1B], [b_hst])
            CP(rtc.xT[:, :, LS:LS + 2], hst[:, :].rearrange("p (k t) -> p k t", t=2), [b_hst], [rtc.b_xT])
            GATHER(hstb[:, :], MIXH, 2, [b_MIXB], [b_hst])
            CP(rtc.hT[:, :, LS:LS + 2], hstb[:, :].rearrange("p (k t) -> p k t", t=2), [b_hst], [rtc.b_hT])
            for j in range(2):
                DMA("sp", rtc.cx[:, :, j], st_sc[j].rearrange("(k p) -> p k", p=128), [], [rtc.b_cx], slow=True)

        def lin_resid(W, l, j, src):
            def mk(tc):
                T = tc.T
                use(tc)
                buf = tc.hT if src == "h" else tc.aT
                bb = tc.b_hT if src == "h" else tc.b_aT
                return (lambda kc: buf[:, kc, 0:T]), [bb], T, resid_evac(l, j, tc.segs, T)
            parts = [mk(tc) for tc in tcs]
            linear_fm(W, groups8, [(0, KD)], *parts[0], extra=parts[1:])

        def all_norm(l, j):
            for tc in tcs:
                use(tc)
                norm_mod(l, j, tc.segs, tc.T)

        lin_resid(w_out0, 0, 1, "h")
        all_norm(0, 2)
        ffn(0, 1, 2, tcs)
        all_norm(1, 0)
        ffn(1, 0, 0, tcs)
        all_norm(1, 1)
        groups = [[(D + m * 256, 256)] for m in range(KD)]

        def mk_cx(tc):
            T = tc.T

            def ev_cx(gi, ci, rows, p_ap, p_b):
                bi = gi % 2
                if ci == 0:
                    ACP(tc.sg[bi][:, 0:T], p_ap, [p_b], [tc.sg_b[bi]])
                else:
                    TT_(tc.cx[:, gi, 2:2 + T], p_ap, tc.sg[bi][:, 0:T], ALU.mult, [p_b, tc.sg_b[bi]], [tc.b_cx])
            return (lambda kc: tc.hT[:, kc, 0:T]), [tc.b_hT], T, ev_cx
        parts = [mk_cx(tc) for tc in tcs]
        linear_fm(sc_w_in[8:24], groups, [(0, KD)], *parts[0], extra=parts[1:])
        if ti_ == 0:
            TS(mtc.cx[:, :, 0:2], rtc.cx[:, :, 2 + LS:2 + LS + 2], hflag[:, 0:1], None, ALU.mult, None,
               [rtc.b_cx, b_const], [mtc.b_cx])

        def mk_bg(tc):
            T = tc.T
            cx, b_cx, ytmp, b_ytmp = tc.cx, tc.b_cx, tc.yt, tc.b_yt

            def ev_bg(gi, ci, rows, p_ap, p_b):
                dch = gi * 2 + ci
                TS(ytmp[:, 0:T], cx[:, dch, 0:T], scw[:, 0, dch:dch + 1], None, ALU.mult, None, [b_cx, b_const], [b_ytmp])
                STT(ytmp[:, 0:T], cx[:, dch, 1:1 + T], scw[:, 1, dch:dch + 1], ytmp[:, 0:T], ALU.mult, ALU.add,
                    [b_cx, b_const, b_ytmp], [b_ytmp])
                STT(ytmp[:, 0:T], cx[:, dch, 2:2 + T], scw[:, 2, dch:dch + 1], ytmp[:, 0:T], ALU.mult, ALU.add,
                    [b_cx, b_const, b_ytmp], [b_ytmp])
                TT_(tc.aT[:, dch, 0:T], p_ap, ytmp[:, 0:T], ALU.mult, [p_b, b_ytmp], [tc.b_aT])
            return (lambda kc: tc.hT[:, kc, 0:T]), [tc.b_hT], T, ev_bg
        parts = [mk_bg(tc) for tc in tcs]
        linear_fm(sc_w_in[0:8], groups8, [(0, KD)], *parts[0], extra=parts[1:])
        if ti_ == 0:
            for j in range(2):
                DMA("sp", o_sc[1][j].rearrange("(k p) -> p k", p=128), rtc.cx[:, :, LS + j], [rtc.b_cx], [b_out], slow=True)
        if ti_ == nown - 1:
            for j in range(2):
                DMA("sp", o_sc[0][j].rearrange("(k p) -> p k", p=128), mtc.cx[:, :, TT + j], [mtc.b_cx], [b_out], slow=True)
        else:
            CP(sg[0][:, 0:32].rearrange("p (k j) -> p k j", j=2), mtc.cx[:, :, TT:TT + 2], [mtc.b_cx], [sg_b[0]])
            CP(mtc.cx[:, :, 0:2], sg[0][:, 0:32].rearrange("p (k j) -> p k j", j=2), [sg_b[0]], [mtc.b_cx])
        lin_resid(sc_w_out, 1, 1, "a")
        all_norm(1, 2)
        ffn(1, 1, 2, tcs)
        for tc in tcs:
            use(tc)
            xT, b_xT, T = tc.xT, tc.b_xT, tc.T
            rms_stats(T, D)
            for k in range(KD):
                STT(xT[:, k, 0:T], xT[:, k, 0:T], lnT[:, 6, k:k + 1], rstd[:, 0:T], ALU.mult, ALU.mult,
                    [b_xT, b_rstd, b_const], [b_xT])
            To = LS if tc.kind == "rider" else T
            so = 1 if tc.kind == "rider" else 0
            r0 = 0 if tc.kind == "rider" else ti_ * TT
            ntb = (To + 127) // 128
            for tb in range(ntb):
                n = min(128, To - tb * 128)
                bi = tb % 2
                for k4 in range(4):
                    pb, pbb = misc_bank()
                    for kk in range(4):
                        k = k4 * 4 + kk
                        TR(pb[0:n, kk * 128:(kk + 1) * 128], xT[:, k, tb * 128:tb * 128 + n], ident[:, :], [b_xT, b_const], [pbb])
                    CP(xtm[bi][0:n, k4 * 512:(k4 + 1) * 512], pb[0:n, :], [pbb], [xtm_b[bi]])
                DMA("sp", yout[so][r0 + tb * 128:r0 + tb * 128 + n, :], xtm[bi][0:n, :], [xtm_b[bi]], [b_out])

    fw.emit(final_bufs=list(fw.allbufs))
    return nc


def make_consts():
    c = np.zeros((128, NCONST), np.float32)
    j = np.arange(128)[:, None]
    i = np.arange(128)[None, :]
    c[:, 0:128] = (j == i)
    c[:, 128:256] = (j <= i)
    c[:, 256:384] = (j < i)
    c[:, 384:512] = (j >= i)
    c[:, 512:640] = (j > i)
    sm = np.zeros((128, 6, 4), np.float32)
    q = np.arange(4)[None, :]
    sm[:, 0, :] = (np.arange(128)[:, None] >= q)
    for x in range(4):
        sm[:, 1 + x, x] = 1.0
    sm[0:4, 5, :] = (np.arange(4)[:, None] <= q)
    c[:, 640:664] = sm.reshape(128, 24)
    lm = np.zeros((128, 12, 64), np.float32)
    ii = np.arange(64)[:, None]
    jj = np.arange(64)[None, :]
    for li, bsz in enumerate((1, 2, 4, 8, 16, 32)):
        m = (((ii // bsz) % 2) == 1) & ((jj // bsz) == (ii // bsz) - 1)
        lm[0:64, li, :] = m
        lm[0:64, 6 + li, :] = m.T
    c[:, 664:1432] = lm.reshape(128, 768)
    return c


_CACHE = {}


def make_in_maps(inp):
    f = lambda a: np.ascontiguousarray(np.asarray(a, dtype=np.float32))
    lns = np.stack([inp["ln_ffn1"][0], inp["ln_mix"][0], inp["ln_ffn2"][0], inp["ln_ffn1"][1], inp["ln_mix"][1],
                    inp["ln_ffn2"][1], inp["ln_final"]]).astype(np.float32)
    def blk(W, cols, kparts):
        Kc = W.shape[0] // 128
        out = []
        for (c0, n) in cols:
            Wg = W[:, c0:c0 + n].reshape(Kc, 128, n)
            for (k0, nk) in kparts:
                out.append(Wg[k0:k0 + nk].transpose(1, 0, 2).reshape(128, nk * n))
        return np.ascontiguousarray(np.stack(out))

    g256 = lambda m0, cnt, off=0: [(off + m * 256, 256) for m in range(m0, m0 + cnt)]
    kfull = [(0, KD)]
    wu = np.asarray(inp["ffn_w_up"], dtype=np.float32)
    wu = np.stack([wu[..., :FF].reshape(2, 2, D, FC, 128), wu[..., FF:].reshape(2, 2, D, FC, 128)], axis=4).reshape(2, 2, D, 2 * FF)
    wu_b = np.stack([np.stack([blk(wu[l, i], g256(0, FC), kfull) for i in range(2)]) for l in range(2)])
    wd = np.asarray(inp["ffn_w_down"], dtype=np.float32)
    wd_b = np.stack([np.stack([blk(wd[l, i], g256(0, 8), [(0, 11), (11, 11), (22, 11), (33, 11)]) for i in range(2)])
                     for l in range(2)])
    aw = np.asarray(inp["ada_w"], dtype=np.float32)
    aw_b = np.stack([blk(aw[l], g256(0, 72), kfull) for l in range(2)])
    wi = np.asarray(inp["w_in0"][0], dtype=np.float32)
    wi_b = blk(wi, g256(0, 20) + g256(0, 6, 5140) + [(5120, 256)] + g256(0, 6, 5908), kfull)
    sw = np.asarray(inp["sc_w_in"][0], dtype=np.float32)
    sw = np.concatenate(
        [sw[:, :D], np.stack([sw[:, D:2 * D].reshape(D, KD, 128), sw[:, 2 * D:].reshape(D, KD, 128)], axis=2).reshape(D, 2 * D)],
        axis=1)
    sw_b = blk(sw, g256(0, 8) + g256(0, 16, D), kfull)
    wo_b = blk(np.asarray(inp["w_out0"][0], dtype=np.float32), g256(0, 8), kfull)
    so_b = blk(np.asarray(inp["sc_w_out"][0], dtype=np.float32), g256(0, 8), kfull)
    shared = {"ada_w": aw_b, "ada_b": f(inp["ada_b"]), "lns": lns, "ffn_w_up": wu_b,
              "ffn_w_down": wd_b, "w_in0": wi_b, "gdn_conv_w": f(inp["gdn_conv_w"][0]),
              "gdn_a_log": f(inp["gdn_a_log"][0]), "gdn_dt_bias": f(inp["gdn_dt_bias"][0]), "gdn_onorm": f(inp["gdn_onorm"][0]),
              "w_out0": wo_b, "sc_w_in": sw_b, "sc_conv_w": f(inp["sc_conv_w"][0]),
              "sc_w_out": so_b, "consts": make_consts()}
    in_maps = []
    for c in range(8):
        m = dict(shared)
        m["xp"] = f(inp["x_prompt"][c // 4])
        m["xs"] = f(inp["x_sample"][c])
        m["cvec"] = np.stack([inp["c_prompt"][c // 4], inp["c_sample"][c]]).astype(np.float32)
        m["st_S"] = f(inp["state_gdn_S"][0, c])
        m["st_conv"] = f(inp["state_gdn_conv"][0, c])
        for g in range(3):
            m[f"st_kv{g}"] = f(inp[f"cache_swa_kv_g{g}"][0, c])
        m["st_sc"] = f(inp["state_sconv"][0, c])
        r = c % 4
        p = np.arange(128)
        m["gidx"] = np.stack([2 * r * 128 + p, (2 * r + 1) * 128 + p, max(2 * r - 1, 0) * 128 + p], 1).astype(np.uint32)
        m["hflag"] = np.full((128, 1), 0.0 if r == 0 else 1.0, np.float32)
        in_maps.append(m)
    return in_maps


def kernel(**inp):
    if "nc" not in _CACHE:
        _CACHE["nc"] = build()
    nc = _CACHE["nc"]
    in_maps = make_in_maps(inp)
    res = run_bass_kernel_spmd(nc, in_maps, core_ids=list(range(8))).results
    pc = [0, 4]
    y_p = np.stack([np.concatenate([res[4 * b + r]["y_p"] for r in range(4)], 0) for b in range(2)])
    y_s = np.stack([res[c]["y_s"] for c in range(8)])
    outs = [y_p, y_s]
    for grp, cores in (("p", pc), ("s", list(range(8)))):
        outs.append(np.stack([res[c][f"o_S_{grp}"] for c in cores])[None])
        outs.append(np.stack([res[c][f"o_conv_{grp}"] for c in cores])[None])
        for g in range(3):
            outs.append(np.stack([res[c][f"o_kv{g}_{grp}"] for c in cores])[None])
        outs.append(np.stack([res[c + 3 if grp == "p" else c][f"o_sc_{grp}"] for c in cores])[None])
    return tuple(np.ascontiguousarray(o, dtype=np.float32) for o in outs)
```

```python
import contextlib
import numpy as np
import concourse.bass as bass
import concourse.mybir as mybir
from concourse.bass_utils import run_bass_kernel_spmd

F32 = mybir.dt.float32
BF16 = mybir.dt.bfloat16
AF = mybir.ActivationFunctionType
ALU = mybir.AluOpType

D = 2048
KD = 16
FF = 5632
FC = 44
LP = 4096
LS = 4
TT = 512
EPS = 1e-6

EPOCH = 12000
NDS = 12
ENGS = ("pe", "act", "dve", "pool", "sp")
SCHEDULE = True
SAME_ENGINE_SYNC = {"pe": False, "act": True, "dve": True, "pool": True, "sp": False}


class Buf:
    __slots__ = ("name", "w", "r")

    def __init__(self, name):
        self.name = name
        self.w = None
        self.r = []


class Op:
    __slots__ = ("eng", "idx", "fn", "deps", "is_dma", "ref", "semval", "dslot", "dval", "seq", "cost", "lat", "users",
                 "nrem", "ready", "fin")

    def __init__(self, eng, idx, fn, is_dma):
        self.eng, self.idx, self.fn, self.is_dma = eng, idx, fn, is_dma
        self.seq = 0
        self.cost = 0.3
        self.lat = 0.0
        self.users = []
        self.nrem = 0
        self.ready = 0.0
        self.fin = 0.0
        self.deps = []
        self.ref = False
        self.semval = None
        self.dslot = None
        self.dval = None


class FW:
    def __init__(self, nc):
        self.nc = nc
        self.ops = {e: [] for e in ENGS}
        self.ndma = {e: 0 for e in ENGS}
        self.stack = contextlib.ExitStack()
        self.nbuf = 0
        self.nseq = 0
        self.allbufs = []
        self.last_fence = None

    def sbuf(self, name, shape, dt):
        return self.stack.enter_context(self.nc.sbuf_tensor(name, list(shape), dt))

    def psum(self, name, shape, dt=F32):
        return self.stack.enter_context(self.nc.psum_tensor(name, list(shape), dt))

    def buf(self, name=None):
        self.nbuf += 1
        b = Buf(name or f"b{self.nbuf}")
        b.w = self.last_fence
        self.allbufs.append(b)
        return b

    def fence(self, scratch_ap):
        self.last_fence = self.op("dve", lambda e: e.memset(scratch_ap, 0.0), reads=(), writes=list(self.allbufs))

    def op(self, eng, fn, reads=(), writes=(), dma=False, cost=None, lat=0.0):
        lst = self.ops[eng]
        o = Op(eng, len(lst), fn, dma)
        self.nseq += 1
        o.seq = self.nseq
        o.cost = cost if cost is not None else (0.06 if dma else 0.3)
        o.lat = lat if lat else (3.0 if dma else 0.0)
        deps = []
        for b in reads:
            if b.w is not None:
                deps.append(b.w)
        for b in writes:
            if b.w is not None:
                deps.append(b.w)
            deps.extend(b.r)
        o.deps = deps
        if dma:
            self.ndma[eng] += 1
        lst.append(o)
        for b in writes:
            b.w = o
            b.r = []
        for b in reads:
            if b.w is not o:
                b.r.append(o)
        return o

    def pe(self, fn, reads=(), writes=(), cost=None):
        return self.op("pe", fn, reads, writes, cost=cost)

    def act(self, fn, reads=(), writes=(), cost=None):
        return self.op("act", fn, reads, writes, cost=cost)

    def dve(self, fn, reads=(), writes=(), cost=None):
        return self.op("dve", fn, reads, writes, cost=cost)

    def dma(self, q, out, in_, reads=(), writes=(), **kw):
        nb = 1
        for x in out.shape:
            nb *= x
        lat = 2.5 + nb * 4 / 150e3
        return self.op(q, lambda e: e.dma_start(out=out, in_=in_, **kw), reads, writes, dma=True, lat=lat)

    def schedule(self):
        import heapq
        allops = []
        for e in ENGS:
            allops.extend(self.ops[e])
        for o in allops:
            o.users = []
        for o in allops:
            ds = set(id(d) for d in o.deps)
            uniq = {}
            for d in o.deps:
                uniq[id(d)] = d
            o.deps = list(uniq.values())
            o.nrem = len(o.deps)
            for d in o.deps:
                d.users.append(o)
        readyq = {e: [] for e in ENGS}
        free_at = {e: 0.0 for e in ENGS}
        busy = {e: False for e in ENGS}
        order = {e: [] for e in ENGS}
        events = []
        SYNC = 0.15

        def try_start(e, now):
            if busy[e] or not readyq[e]:
                return
            seq, _, o = heapq.heappop(readyq[e])
            st = max(now, free_at[e], o.ready)
            o.fin = st + o.cost
            free_at[e] = o.fin
            busy[e] = True
            order[e].append(o)
            heapq.heappush(events, (o.fin, o.seq, 0, o))
            heapq.heappush(events, (o.fin + o.lat, o.seq, 1, o))

        for o in allops:
            if o.nrem == 0:
                heapq.heappush(readyq[o.eng], (o.seq, id(o), o))
        for e in ENGS:
            try_start(e, 0.0)
        nsched = 0
        while events:
            t, _, kind, o = heapq.heappop(events)
            if kind == 0:
                busy[o.eng] = False
                try_start(o.eng, t)
            else:
                nsched += 1
                for u in o.users:
                    u.nrem -= 1
                    if u.ready < t + SYNC:
                        u.ready = t + SYNC
                    if u.nrem == 0:
                        heapq.heappush(readyq[u.eng], (u.seq, id(u), u))
                        try_start(u.eng, t)
        assert nsched == len(allops), (nsched, len(allops))
        self.sim_time = max(free_at.values())
        for e in ENGS:
            self.ops[e] = order[e]
            nd = 0
            for i, o in enumerate(order[e]):
                o.idx = i
                if o.is_dma:
                    o.dslot = nd % NDS
                    o.dval = 16 * (nd // NDS + 1)
                    nd += 1

    def emit(self, final_bufs=()):
        nc = self.nc
        self.op("sp", None, reads=list(final_bufs), writes=(), cost=0.01)
        if SCHEDULE:
            self.schedule()
        else:
            for e in ENGS:
                nd = 0
                for o in self.ops[e]:
                    if o.is_dma:
                        o.dslot = nd % NDS
                        o.dval = 16 * (nd // NDS + 1)
                        nd += 1
        for e in ENGS:
            for o in self.ops[e]:
                for d in o.deps:
                    if not d.is_dma:
                        if d.eng != o.eng or SAME_ENGINE_SYNC[o.eng] or o.is_dma:
                            d.ref = True
        nsem_eng = {}
        for e in ENGS:
            c = 0
            for o in self.ops[e]:
                if o.ref and not o.is_dma:
                    c += 1
                    o.semval = c
            nsem_eng[e] = c // EPOCH + 1
        sems = {}
        for e in ENGS:
            sems[e] = [self.stack.enter_context(nc.semaphore(f"s_{e}_{i}")) for i in range(nsem_eng[e])]
        dsems = {}
        for e in ENGS:
            if self.ndma[e]:
                dsems[e] = [self.stack.enter_context(nc.semaphore(f"d_{e}_{i}")) for i in range(NDS)]

        def replay(ename, eh):
            known = {}
            for o in self.ops[ename]:
                waits = {}
                for d in o.deps:
                    if d.is_dma:
                        key = ("d", d.eng, d.dslot)
                        val = d.dval
                    else:
                        if d.eng == ename and not (SAME_ENGINE_SYNC[ename] or o.is_dma):
                            continue
                        key = ("e", d.eng)
                        val = d.semval
                    if known.get(key, 0) >= val:
                        continue
                    if waits.get(key, 0) < val:
                        waits[key] = val
                if o.is_dma:
                    key = ("d", ename, o.dslot)
                    val = o.dval - 16
                    if val > 0 and known.get(key, 0) < val and waits.get(key, 0) < val:
                        waits[key] = val
                for key, val in waits.items():
                    known[key] = val
                    if key[0] == "d":
                        eh.wait_ge(dsems[key[1]][key[2]], val)
                    else:
                        v0 = val - 1
                        eh.wait_ge(sems[key[1]][v0 // EPOCH], v0 % EPOCH + 1)
                if o.fn is None:
                    continue
                inst = o.fn(eh)
                if o.is_dma:
                    inst.then_inc(dsems[ename][o.dslot], 16)
                elif o.ref:
                    v0 = o.semval - 1
                    inst.then_inc(sems[ename][v0 // EPOCH], 1)

        with nc.Block() as block:
            @block.tensor
            def _(e):
                replay("pe", e)

            @block.scalar
            def _(e):
                replay("act", e)

            @block.vector
            def _(e):
                replay("dve", e)

            @block.gpsimd
            def _(e):
                replay("pool", e)

            @block.sync
            def _(e):
                replay("sp", e)
        self.stack.close()


WS_ELEMS = 4096
NSLOT = 4
ARENA = 32000
NCONST = 1432
SWA_CFG = ((128, 1), (512, 4), (2048, 16))
IN0 = 7444


class Ctx:
    pass


def build(dbg=()):
    nc = bass.Bass("TRN2", target_bir_lowering=False)
    fw = FW(nc)
    C = Ctx()

    def din(name, shape, dt=F32):
        return nc.dram_tensor(name, list(shape), dt, kind="ExternalInput").ap()

    def dout(name, shape, dt=F32):
        return nc.dram_tensor(name, list(shape), dt, kind="ExternalOutput").ap()

    def dscr(name, shape, dt=F32):
        kind = "ExternalOutput" if name in dbg else "Internal"
        return nc.dram_tensor(name, list(shape), dt, kind=kind).ap()

    L = [LP, LS]
    CH = [64, 4]
    xin = [din("xp", [LP, D]), din("xs", [LS, D])]
    cvec = din("cvec", [2, D])
    ada_w = din("ada_w", [2, D, 9 * D])
    ada_b = din("ada_b", [2, 9 * D])
    lns = din("lns", [7, D])
    w_up = din("ffn_w_up", [2, 2, D, 2 * FF])
    w_dn = din("ffn_w_down", [2, 2, FF, D])
    w_in0 = din("w_in0", [D, IN0])
    gconvw = din("gdn_conv_w", [4, 3840])
    galog = din("gdn_a_log", [10])
    gdtb = din("gdn_dt_bias", [10])
    gonorm = din("gdn_onorm", [128])
    w_out0 = din("w_out0", [D, D])
    sc_w_in = din("sc_w_in", [D, 3 * D])
    sc_conv_w = din("sc_conv_w", [3, D])
    sc_w_out = din("sc_w_out", [D, D])
    consts_d = din("consts", [128, NCONST])
    st_S = din("st_S", [10, 128, 128])
    st_conv = din("st_conv", [3, 3840])
    st_kv = [din(f"st_kv{g}", [SWA_CFG[g][0], 2, 4, 64]) for g in range(3)]
    st_sc = din("st_sc", [2, D])
    gidx_d = din("gidx", [128, 3], mybir.dt.uint32)
    hflag_d = din("hflag", [128, 1])
    OWN = 1024
    yout = [dout("y_p", [OWN, D]), dout("y_s", [LS, D])]
    o_S = [dout("o_S_p", [10, 128, 128]), dout("o_S_s", [10, 128, 128])]
    o_conv = [dout("o_conv_p", [3, 3840]), dout("o_conv_s", [3, 3840])]
    o_kv = [[dout(f"o_kv{g}_p", [SWA_CFG[g][0], 2, 4, 64]) for g in range(3)],
            [dout(f"o_kv{g}_s", [SWA_CFG[g][0], 2, 4, 64]) for g in range(3)]]
    o_sc = [dout("o_sc_p", [2, D]), dout("o_sc_s", [2, D])]
    X1T = [dscr("x1T_p", [D, 4]), dscr("x1T_s", [D, LS])]
    NTL = LP // TT
    X1B = dscr("x1B", [NTL * 128, KD * TT])
    X1H = dscr("x1H", [NTL * 128, KD * 2])
    MIXB = dscr("mixB", [NTL * 128, KD * TT], BF16)
    MIXH = dscr("mixH", [NTL * 128, KD * 2], BF16)
    b_X1B, b_MIXB = fw.buf("x1b"), fw.buf("mixb")
    PJT = [dscr("pjT_p", [6656, LP]), dscr("pjT_s", [6656, LS])]
    ABTM = [dscr("ab_p", [LP, 20]), dscr("ab_s", [LS, 20])]
    KVTM = [dscr("kv_p", [LP, 1536]), dscr("kv_s", [LS, 1536])]
    MIXT = [dscr("mixT_p", [D, LP], BF16), dscr("mixT_s", [D, LS], BF16)]
    OB = [dscr("ob_p", [LP, 12, 65]), dscr("ob_s", [LS, 12, 65])]

    arena = fw.sbuf("arena", [128, ARENA], F32)
    wsl = [fw.sbuf(f"wsl{i}", [128, WS_ELEMS], BF16) for i in range(NSLOT)]
    wsl_b = [fw.buf(f"wsl{i}") for i in range(NSLOT)]
    xtm = [fw.sbuf(f"xtm{i}", [128, D], F32) for i in range(2)]
    xtm_b = [fw.buf() for _ in range(2)]
    consts = fw.sbuf("consts_sb", [128, NCONST], F32)
    ident = consts[:, 0:128]
    triU = consts[:, 128:256]
    triUs = consts[:, 256:384]
    triL = consts[:, 384:512]
    triLs = consts[:, 512:640]
    smask = consts[:, 640:664].rearrange("p (a b) -> p a b", b=4)
    lmask = consts[:, 664:1432].rearrange("p (a b) -> p a b", b=64)
    ones = fw.sbuf("ones", [128, 128], F32)
    modT = fw.sbuf("modT", [128, 2, 144, 2], F32)
    Amod = fw.sbuf("Amod", [128, 2, 3, KD, 2], F32)
    Gmod = fw.sbuf("Gmod", [128, 2, 3, KD, 2], F32)
    lnT = fw.sbuf("lnT", [128, 7, KD], F32)
    adabT = fw.sbuf("adabT", [128, 2, 144], F32)
    cT = fw.sbuf("cT", [128, KD, 2], F32)
    csT = fw.sbuf("csT", [128, KD, 2], BF16)
    scw = fw.sbuf("scw", [128, 3, KD], F32)
    gcw = fw.sbuf("gcw", [128, 4, 30], F32)
    alb = fw.sbuf("alb", [128, 10], F32)
    dtb = fw.sbuf("dtb", [128, 10], F32)
    onrm = fw.sbuf("onrm", [128, 1], F32)
    epsT = fw.sbuf("epsT", [128, 1], F32)
    fsc = fw.sbuf("fsc", [128, 8], F32)
    gidx = fw.sbuf("gidx_sb", [128, 3], mybir.dt.uint32)
    hflag = fw.sbuf("hflag_sb", [128, 1], F32)
    hst = fw.sbuf("hst", [128, KD * 2], F32)
    hstb = fw.sbuf("hstb", [128, KD * 2], BF16)
    halo = fw.sbuf("halo", [128, KD, 2], F32)
    b_hst, b_halo = fw.buf("hst"), fw.buf("halo")
    sq = [fw.sbuf(f"sq{i}", [128, TT], F32) for i in range(2)]
    sq_b = [fw.buf() for _ in range(2)]
    rstd = fw.sbuf("rstd", [128, TT], F32)
    tmpn = [fw.sbuf(f"tmpn{i}", [128, TT], F32) for i in range(2)]
    tmpn_b = [fw.buf() for _ in range(2)]
    sg = [fw.sbuf(f"sg{i}", [128, TT], F32) for i in range(2)]
    sg_b = [fw.buf() for _ in range(2)]
    b_const, b_mod, b_rstd, b_cs = (fw.buf(n) for n in ("const", "mod", "rstd", "cs"))
    b_X1T = [fw.buf(), fw.buf()]
    b_PJT = [fw.buf(), fw.buf()]
    b_AB = [fw.buf(), fw.buf()]
    b_KV = [fw.buf(), fw.buf()]
    b_MIX = [fw.buf(), fw.buf()]
    b_OB = [fw.buf(), fw.buf()]
    b_out = fw.buf("out")
    psall = fw.psum("psall", [128, 8, 512])
    ps = [psall[:, i, :] for i in range(8)]
    ps_b = [fw.buf(f"ps{i}") for i in range(8)]
    C.lin_rr = 0
    C.slot_rr = 0
    C.misc_rr = 0
    C.pb_rr = 0
    C.aoff = 0

    def misc_bank():
        i = 6 + (C.misc_rr % 2)
        C.misc_rr += 1
        return ps[i], ps_b[i]

    def pbank():
        i = C.pb_rr % 8
        C.pb_rr += 1
        return ps[i], ps_b[i]

    def carve(shape, dt):
        n = 1
        for x in shape[1:]:
            n *= x
        nbytes = n * (4 if dt == F32 else 2)
        nw = (nbytes + 31) // 32 * 8
        assert C.aoff + nw <= ARENA, ("arena overflow", C.aoff, nw)
        v = arena[0:shape[0], C.aoff:C.aoff + nw]
        C.aoff += nw
        if dt != F32:
            v = v.bitcast(dt)
        v = v[:, 0:n]
        if len(shape) == 3:
            v = v.rearrange("p (a b) -> p a b", b=shape[2])
        elif len(shape) == 4:
            v = v.rearrange("p (a b c) -> p a b c", b=shape[2], c=shape[3])
        return v, fw.buf()

    def stage_reset():
        fw.fence(fsc[:, 0:1])
        C.aoff = 0

    def fsz(ap):
        n = 1
        for x in ap.shape[1:]:
            n *= x
        return n

    def MM(o, l, r, f, la, reads, writes):
        c = (max(64, fsz(r)) / 2400.0) * (4.0 if r.dtype == F32 else 1.0) + 0.02
        fw.pe(lambda e: e.matmul(o, l, r, start=f, stop=la), reads, writes, cost=c)

    def TR(o, i, idn, reads, writes):
        fw.pe(lambda e: e.transpose(o, i, idn), reads, writes, cost=0.15)

    def ACT(out, in_, func, reads, writes, scale=None, bias=None):
        kw = {}
        if scale is not None:
            kw["scale"] = scale
        if bias is not None:
            kw["bias"] = bias
        fw.act(lambda e: e.activation(out=out, in_=in_, func=func, **kw), reads, writes, cost=0.2 + fsz(out) / 1200.0)

    def TT_(out, in0, in1, op, reads, writes):
        fw.dve(lambda e: e.tensor_tensor(out=out, in0=in0, in1=in1, op=op), reads, writes, cost=0.12 + fsz(out) / 960.0)

    def TS(out, in0, s1, s2, op0, op1, reads, writes):
        if op1 is None:
            fw.dve(lambda e: e.tensor_scalar(out=out, in0=in0, scalar1=s1, scalar2=None, op0=op0), reads, writes,
                   cost=0.12 + fsz(out) / 960.0)
        else:
            fw.dve(lambda e: e.tensor_scalar(out=out, in0=in0, scalar1=s1, scalar2=s2, op0=op0, op1=op1), reads, writes,
                   cost=0.12 + fsz(out) / 960.0)

    def STT(out, in0, scalar, in1, op0, op1, reads, writes):
        fw.dve(lambda e: e.scalar_tensor_tensor(out=out, in0=in0, scalar=scalar, in1=in1, op0=op0, op1=op1), reads, writes,
               cost=0.12 + fsz(out) / 960.0)

    def CP(out, in_, reads, writes):
        fw.dve(lambda e: e.tensor_copy(out=out, in_=in_), reads, writes, cost=0.12 + fsz(out) / 960.0)

    def ACP(out, in_, reads, writes):
        fw.act(lambda e: e.copy(out=out, in_=in_), reads, writes, cost=0.2 + fsz(out) / 1200.0)

    def MEMSET(ap, val, writes):
        fw.dve(lambda e: e.memset(ap, val), (), writes)

    def RECIP(out, in_, reads, writes):
        fw.dve(lambda e: e.reciprocal(out=out, in_=in_), reads, writes, cost=0.12 + fsz(out) / 960.0)

    def DMA(q, out, in_, reads, writes, slow=False):
        if slow:
            fw.dma(q, out, in_, reads, writes, allow_slow_non_contiguous=True)
        else:
            fw.dma(q, out, in_, reads, writes)

    def bc(ap, shape):
        return ap.to_broadcast(shape)

    DMA("sp", consts[:], consts_d[:, :], [], [b_const])
    DMA("sp", gidx[:], gidx_d[:, :], [], [b_const])
    DMA("sp", hflag[:], hflag_d[:, :], [], [b_const])

    def GATHER(out2d, src2d, col, reads, writes):
        fw.op("pool", lambda e: e.indirect_dma_start(out=out2d, out_offset=None, in_=src2d,
                                                     in_offset=bass.IndirectOffsetOnAxis(ap=gidx[:, col:col + 1], axis=0)),
              list(reads) + [b_const], writes, dma=True)
    MEMSET(ones[:], 1.0, [b_const])
    MEMSET(epsT[:], EPS, [b_const])
    for j in range(7):
        DMA("sp", lnT[:, j, :], lns[j].rearrange("(k p) -> p k", p=128), [], [b_const], slow=True)
    for l in range(2):
        DMA("sp", adabT[:, l, :], ada_b[l].rearrange("(m p) -> p m", p=128), [], [b_const], slow=True)
    for s in range(2):
        DMA("sp", cT[:, :, s], cvec[s].rearrange("(k p) -> p k", p=128), [], [b_const], slow=True)
    for j in range(3):
        DMA("sp", scw[:, j, :], sc_conv_w[j].rearrange("(k p) -> p k", p=128), [], [b_const], slow=True)
    for j in range(4):
        DMA("sp", gcw[:, j, :], gconvw[j].rearrange("(k p) -> p k", p=128), [], [b_const], slow=True)
    DMA("sp", alb[:], galog.partition_broadcast(128), [], [b_const])
    DMA("sp", dtb[:], gdtb.partition_broadcast(128), [], [b_const])
    DMA("sp", onrm[:], gonorm.rearrange("(p o) -> p o", o=1), [], [b_const], slow=True)
    ACT(csT[:], cT[:], AF.Silu, [b_const], [b_cs])
    nea = fw.sbuf("nea", [128, 10], F32)
    ACT(nea[:], alb[:], AF.Exp, [b_const], [b_const])
    TS(nea[:], nea[:], -1.0, None, ALU.mult, None, [b_const], [b_const])

    def load_wtile(W, k0c, nkc, segs):
        si = C.slot_rr % NSLOT
        C.slot_rr += 1
        GC = sum(n for _, n in segs)
        assert nkc * GC <= WS_ELEMS, (nkc, GC)
        view = wsl[si][:, 0:nkc * GC].rearrange("p (k c) -> p k c", c=GC)
        c0 = 0
        Wv = W.rearrange("(kc p) c -> p kc c", p=128)
        for (col, n) in segs:
            DMA("pool", view[:, :, c0:c0 + n], Wv[:, k0c:k0c + nkc, col:col + n], [], [wsl_b[si]])
            c0 += n
        return si, view

    def linear_fm(W, groups, kparts, rhs_fn, rhs_bufs, T, evac, extra=()):
        tiles = [(gi, kp) for gi in range(len(groups)) for kp in range(len(kparts))]
        loaded = {}
        PF = NSLOT - 2
        st = {"nxt": 0}

        def ensure(upto):
            while st["nxt"] < len(tiles) and st["nxt"] <= upto:
                gi, kp = tiles[st["nxt"]]
                loaded[st["nxt"]] = load_wtile(W, kparts[kp][0], kparts[kp][1], groups[gi])
                st["nxt"] += 1

        ti = 0
        for gi, segs in enumerate(groups):
            GC = sum(n for _, n in segs)
            nch = (GC + 127) // 128
            assert nch <= 2
            base = (C.lin_rr % 3) * 2
            C.lin_rr += 1
            for kp, (k0c, nkc) in enumerate(kparts):
                ensure(ti + PF)
                si, view = loaded.pop(ti)
                ti += 1
                for ci in range(nch):
                    rows = min(128, GC - ci * 128)
                    for kl in range(nkc):
                        kc = k0c + kl
                        first = (kp == 0 and kl == 0)
                        last = (kp == len(kparts) - 1 and kl == nkc - 1)
                        MM(ps[base + ci][0:rows, 0:T], view[:, kl, ci * 128:ci * 128 + rows], rhs_fn(kc), first, last,
                           [wsl_b[si]] + (rhs_bufs(kc) if callable(rhs_bufs) else list(rhs_bufs)), [ps_b[base + ci]])
                        for (rf_e, rb_e, T_e, ev_e) in extra:
                            MM(ps[6 + ci][0:rows, 0:T_e], view[:, kl, ci * 128:ci * 128 + rows], rf_e(kc), first, last,
                               [wsl_b[si]] + (rb_e(kc) if callable(rb_e) else list(rb_e)), [ps_b[6 + ci]])
            for ci in range(nch):
                rows = min(128, GC - ci * 128)
                evac(gi, ci, rows, ps[base + ci][0:rows, 0:T], ps_b[base + ci])
                for (rf_e, rb_e, T_e, ev_e) in extra:
                    ev_e(gi, ci, rows, ps[6 + ci][0:rows, 0:T_e], ps_b[6 + ci])

    def linear_tm(W, groups, lhs_fn, lhs_bufs, T, evac, extra=()):
        ntb = (T + 127) // 128
        for gi, (col, n) in enumerate(groups):
            si, view = load_wtile(W, 0, KD, [(col, n)])
            for tb in range(ntb):
                nt = min(128, T - tb * 128)
                for kc in range(KD):
                    MM(ps[tb][0:nt, 0:n], lhs_fn(kc, tb, nt), view[:, kc, 0:n], kc == 0, kc == KD - 1,
                       [wsl_b[si]] + (lhs_bufs(kc) if callable(lhs_bufs) else list(lhs_bufs)), [ps_b[tb]])
                evac(gi, tb, nt, ps[tb][0:nt, 0:n], ps_b[tb])
            for (lf_e, lb_e, T_e, ev_e) in extra:
                for kc in range(KD):
                    MM(ps[4][0:T_e, 0:n], lf_e(kc, 0, T_e), view[:, kc, 0:n], kc == 0, kc == KD - 1,
                       [wsl_b[si]] + (lb_e(kc) if callable(lb_e) else list(lb_e)), [ps_b[4]])
                ev_e(gi, 0, T_e, ps[4][0:T_e, 0:n], ps_b[4])

    for l in range(2):
        groups = [[(m * 256, 256)] for m in range(72)]

        def evac_mod(gi, ci, rows, p_ap, p_b, l=l):
            m = gi * 2 + ci
            TS(modT[:, l, m, :], p_ap, adabT[:, l, m:m + 1], None, ALU.add, None, [p_b, b_const], [b_mod])
        linear_fm(ada_w[l], groups, [(0, KD)], lambda kc: csT[:, kc, :], [b_cs], 2, evac_mod)
    for l in range(2):
        for j in range(3):
            for s in range(2):
                STT(Amod[:, l, j, :, s], modT[:, l, (3 * j + 1) * KD:(3 * j + 2) * KD, s], 1.0, lnT[:, 3 * l + j, :],
                    ALU.add, ALU.mult, [b_mod, b_const], [b_mod])
                TS(Gmod[:, l, j, :, s], modT[:, l, (3 * j + 2) * KD:(3 * j + 3) * KD, s], (1.0 if j == 1 else 0.5), None,
                   ALU.mult, None, [b_mod], [b_mod])

    RW = 8
    r_xT = fw.sbuf("r_xT", [128, KD, RW], F32)
    r_hT = fw.sbuf("r_hT", [128, KD, RW], BF16)
    r_aT = fw.sbuf("r_aT", [128, FC, RW], BF16)
    r_cx = fw.sbuf("r_cx", [128, KD, 2 + RW], F32)
    r_sg = [fw.sbuf(f"r_sg{i}", [128, RW], F32) for i in range(2)]
    r_stg = [fw.sbuf(f"r_stg{i}", [128, RW], F32) for i in range(2)]
    r_sq = fw.sbuf("r_sq", [RW, 256], F32)
    r_yt = fw.sbuf("r_yt", [128, RW], F32)

    class TCx:
        pass

    def make_rider():
        tc = TCx()
        tc.xT, tc.hT, tc.aT, tc.cx = r_xT, r_hT, r_aT, r_cx
        tc.b_xT, tc.b_hT, tc.b_aT, tc.b_cx = [fw.buf("rx")] * KD, [fw.buf("rh")] * KD, fw.buf("ra"), fw.buf("rcx")
        tc.sg, tc.sg_b = r_sg, [fw.buf(), fw.buf()]
        tc.stg, tc.stg_b = r_stg, [fw.buf(), fw.buf()]
        tc.sqt, tc.sqt_b = [r_sq, r_sq], [fw.buf("rsq")] * 2
        tc.yt, tc.b_yt = r_yt, fw.buf("ryt")
        return tc

    def carve_tile():
        tc = TCx()
        tc.xT, _ = carve([128, KD, TT], F32)
        tc.hT, _ = carve([128, KD, TT], BF16)
        tc.b_xT = [fw.buf() for _ in range(KD)]
        tc.b_hT = [fw.buf() for _ in range(KD)]
        tc.aT, tc.b_aT = carve([128, FC, TT], BF16)
        tc.cx, tc.b_cx = None, None
        tc.sg, tc.sg_b = sg, sg_b
        tc.stg, tc.stg_b = [tmpn[0], tmpn[1]], tmpn_b
        tc.sqt, tc.sqt_b = sq, sq_b
        tc.yt, tc.b_yt = tmpn[0], tmpn_b[0]
        return tc

    def use(tc):
        C.xT, C.b_xT, C.hT, C.b_hT, C.aT, C.b_aT = tc.xT, tc.b_xT, tc.hT, tc.b_hT, tc.aT, tc.b_aT

    def load_x_tile(s, t0, T):
        xT, b_xT = C.xT, C.b_xT
        ntb = (T + 127) // 128
        for tb in range(ntb):
            n = min(128, T - tb * 128)
            bi = tb % 2
            DMA("sp", xtm[bi][0:n, :], xin[s][t0 + tb * 128:t0 + tb * 128 + n, :], [], [xtm_b[bi]])
            for k4 in range(4):
                pb, pbb = misc_bank()
                for kk in range(4):
                    k = k4 * 4 + kk
                    TR(pb[:, kk * 128:kk * 128 + n], xtm[bi][0:n, k * 128:(k + 1) * 128], ident[0:n, 0:n],
                       [xtm_b[bi], b_const], [pbb])
                CP(xT[:, k4 * 4:k4 * 4 + 4, tb * 128:tb * 128 + n],
                   pb.rearrange("p (k t) -> p k t", t=128)[:, :, 0:n], [pbb], b_xT[k4 * 4:k4 * 4 + 4])

    def rms_stats(T, div):
        xT, b_xT = C.xT, C.b_xT
        pb, pbb = misc_bank()
        for k in range(KD):
            bi = k % 2
            ACT(sq[bi][:, 0:T], xT[:, k, 0:T], AF.Square, [b_xT[k]], [sq_b[bi]])
            MM(pb[:, 0:T], ones[:, :], sq[bi][:, 0:T], k == 0, k == KD - 1, [sq_b[bi], b_const], [pbb])
        ACT(rstd[:, 0:T], pb[:, 0:T], AF.Sqrt, [pbb, b_const], [b_rstd], scale=1.0 / div, bias=epsT[:, 0:1])
        RECIP(rstd[:, 0:T], rstd[:, 0:T], [b_rstd], [b_rstd])

    def norm_mod(l, j, segs, T):
        xT, b_xT, hT, b_hT = C.xT, C.b_xT, C.hT, C.b_hT
        rms_stats(T, D)
        for k in range(KD):
            bi = k % 2
            TT_(tmpn[bi][:, 0:T], xT[:, k, 0:T], rstd[:, 0:T], ALU.mult, [b_xT[k], b_rstd], [tmpn_b[bi]])
            for (s, c0, n) in segs:
                ACT(hT[:, k, c0:c0 + n], tmpn[bi][:, c0:c0 + n], AF.Identity, [tmpn_b[bi], b_mod], [b_hT[k]],
                    scale=Amod[:, l, j, k, s:s + 1], bias=modT[:, l, 3 * j * KD + k, s:s + 1])

    def resid_evac(l, j, segs, T):
        xT, b_xT = C.xT, C.b_xT

        def ev(gi, ci, rows, p_ap, p_b):
            d = gi * 2 + ci
            for (s, c0, n) in segs:
                STT(xT[:, d, c0:c0 + n], p_ap[:, c0:c0 + n], Gmod[:, l, j, d, s:s + 1], xT[:, d, c0:c0 + n], ALU.mult, ALU.add,
                    [p_b, b_mod, b_xT[d]], [b_xT[d]])
        return ev

    def ffn(l, i, j, tcs):
        groups = [[(m * 256, 256)] for m in range(FC)]

        def mk_up(tc):
            T = tc.T

            def evac_up(gi, ci, rows, p_ap, p_b):
                bi = gi % 2
                if ci == 0:
                    ACT(tc.sg[bi][:, 0:T], p_ap, AF.Silu, [p_b], [tc.sg_b[bi]])
                else:
                    TT_(tc.aT[:, gi, 0:T], p_ap, tc.sg[bi][:, 0:T], ALU.mult, [p_b, tc.sg_b[bi]], [tc.b_aT])
            return (lambda kc: tc.hT[:, kc, 0:T]), (lambda kc: [tc.b_hT[kc]]), T, evac_up
        parts = [mk_up(tc) for tc in tcs]
        linear_fm(w_up[l, i], groups, [(0, KD)], *parts[0], extra=parts[1:])
        groups = [[(m * 256, 256)] for m in range(8)]

        def mk_dn(tc):
            T = tc.T
            use(tc)
            return (lambda kc: tc.aT[:, kc, 0:T]), [tc.b_aT], T, resid_evac(l, j, tc.segs, T)
        parts = [mk_dn(tc) for tc in tcs]
        linear_fm(w_dn[l, i], groups, [(0, 11), (11, 11), (22, 11), (33, 11)], *parts[0], extra=parts[1:])

    def in_proj(tcs):
        groups = [[(m * 256, 256)] for m in range(20)] + [[(5140 + m * 256, 256)] for m in range(6)]

        def mk(tc):
            T, s, t0 = tc.T, tc.s, tc.t0

            def ev(gi, ci, rows, p_ap, p_b):
                row0 = gi * 256 + ci * 128
                bi = ci
                if bi == 0:
                    ACP(tc.stg[bi][:, 0:T], p_ap, [p_b], [tc.stg_b[bi]])
                else:
                    CP(tc.stg[bi][:, 0:T], p_ap, [p_b], [tc.stg_b[bi]])
                DMA("sp", PJT[s][row0:row0 + 128, t0:t0 + T], tc.stg[bi][:, 0:T], [tc.stg_b[bi]], [b_PJT[s]])
            return (lambda kc: tc.hT[:, kc, 0:T]), (lambda kc: [tc.b_hT[kc]]), T, ev
        parts = [mk(tc) for tc in tcs]
        linear_fm(w_in0, groups, [(0, KD)], *parts[0], extra=parts[1:])
        tgroups = [(5120, 20)] + [(5908 + m * 256, 256) for m in range(6)]

        def mkt(tc):
            T, s, t0 = tc.T, tc.s, tc.t0

            def evt(gi, tb, nt, p_ap, p_b):
                bi = tb % 2
                n = 20 if gi == 0 else 256
                CP(tc.sqt[bi][0:nt, 0:n], p_ap, [p_b], [tc.sqt_b[bi]])
                if gi == 0:
                    DMA("sp", ABTM[s][t0 + tb * 128:t0 + tb * 128 + nt, :], tc.sqt[bi][0:nt, 0:20], [tc.sqt_b[bi]], [b_AB[s]])
                else:
                    c0 = (gi - 1) * 256
                    DMA("sp", KVTM[s][t0 + tb * 128:t0 + tb * 128 + nt, c0:c0 + 256], tc.sqt[bi][0:nt, 0:256], [tc.sqt_b[bi]],
                        [b_KV[s]])
            return (lambda kc, tb, nt: tc.hT[:, kc, tb * 128:tb * 128 + nt]), (lambda kc: [tc.b_hT[kc]]), T, evt
        partst = [mkt(tc) for tc in tcs]
        linear_tm(w_in0, tgroups, *partst[0], extra=partst[1:])

    def store_xT(s, t0, T):
        if s == 1:
            DMA("sp", X1T[1].rearrange("(k p) t -> p k t", p=128)[:, :, 0:T], C.xT[:, :, 0:T], list(C.b_xT), [b_X1T[1]])
        else:
            tl = t0 // TT
            DMA("sp", X1B[tl * 128:(tl + 1) * 128, :], C.xT[:].rearrange("p k t -> p (k t)"), list(C.b_xT), [b_X1B])
            DMA("sp", X1H[tl * 128:(tl + 1) * 128, :].rearrange("p (k t) -> p k t", t=2), C.xT[:, :, TT - 2:TT], list(C.b_xT), [b_X1B])

    ptiles = [(0, t0, TT) for t0 in range(0, LP, TT)]
    if "short" in dbg:
        ptiles = [(0, t0, TT) for t0 in range(0, 2048, TT)]
        L = [2048, LS]
    stage_reset()
    mtc = carve_tile()
    rtc = make_rider()
    for ti_, (s, t0, T) in enumerate(ptiles):
        mtc.s, mtc.t0, mtc.T, mtc.segs = s, t0, T, [(s, 0, T)]
        tcs = [mtc]
        if ti_ == 0:
            rtc.s, rtc.t0, rtc.T, rtc.segs = 1, 0, LS, [(1, 0, LS)]
            tcs.append(rtc)
        for tc in tcs:
            use(tc)
            load_x_tile(tc.s, tc.t0, tc.T)
            norm_mod(0, 0, tc.segs, tc.T)
        ffn(0, 0, 0, tcs)
        for tc in tcs:
            use(tc)
            store_xT(tc.s, tc.t0, tc.T)
            norm_mod(0, 1, tc.segs, tc.T)
        in_proj(tcs)

    if "stopA" in dbg:
        fw.emit(final_bufs=list(fw.allbufs))
        return nc

    def gdn(s):
        Cc = CH[s]
        nchunks = L[s] // Cc
        nsteps = {64: 5, 4: 1}[Cc]
        NB5 = 5 * Cc
        rawT, b_raw = carve([128, 40, 3 + Cc], F32)
        acc, b_acc = carve([128, 30, Cc], F32)
        tmp, b_tmp = carve([128, 30, Cc], F32)
        rn, b_rn = carve([128, 20, Cc], F32)
        qkb, b_qkb = carve([128, 20, Cc], BF16)
        grow, b_grow = carve([128, 10, Cc], F32)
        brow, b_brow = carve([128, 10, Cc], F32)
        egr, b_egr = carve([128, 10, Cc], F32)
        bge, b_bge = carve([128, 10, Cc], F32)
        rhsg, b_rhsg = carve([Cc, 10, Cc], F32)
        rhsb, b_rhsb = carve([Cc, 10, Cc], F32)
        dd, b_dd = carve([Cc, 10, Cc], F32)
        decT, b_decT = carve([Cc, 10, Cc], F32)
        decN, b_decN = carve([Cc, 10, Cc], F32)
        t1, b_t1 = carve([Cc, 10, Cc], F32)
        t2, b_t2 = carve([Cc, 10, Cc], F32)
        t3, b_t3 = carve([Cc, 10, Cc], F32)
        Nk = [carve([Cc, 10, Cc], BF16) for _ in range(2)]
        Bk = [carve([Cc, 10, Cc], BF16) for _ in range(2)]
        Wk = [carve([Cc, 10, Cc], BF16) for _ in range(2)]
        Wtk = [carve([Cc, 10, Cc], BF16) for _ in range(2)]
        Boff, b_Boff = carve([Cc, 10, Cc], BF16)
        Noff, b_Noff = carve([Cc, 10, Cc], BF16)
        Xs, b_Xs = carve([Cc, 10, Cc], BF16)
        X2s, b_X2s = carve([Cc, 10, Cc], BF16)
        QKT, b_QKT = carve([Cc, 10, Cc], BF16)
        qdT, b_qdT = carve([128, 10, Cc], BF16)
        kbgT, b_kbgT = carve([128, 10, Cc], BF16)
        kd, b_kd = carve([Cc, 10, 128], BF16)
        vb, b_vb = carve([Cc, 10, 128], F32)
        rb, b_rb = carve([Cc, 10, 128], BF16)
        ub, b_ub = carve([Cc, 10, 128], BF16)
        S, b_S = carve([128, 10, 128], F32)
        Sb, b_Sb = carve([128, 10, 128], BF16)
        oT, b_oT = carve([128, 10, Cc], F32)
        o2, b_o2 = carve([128, 10, Cc], F32)
        sz, b_sz = carve([128, 10, Cc], F32)
        mixa, b_mixa = carve([128, 10, Cc], BF16)
        abt, b_abt = carve([Cc, 20], F32)
        sm, b_sm = carve([Cc, 8, 10], F32)

        if s == 0:
            MEMSET(S[:], 0.0, [b_S])
            MEMSET(rawT[:, :, 0:3], 0.0, [b_raw])
        else:
            DMA("sp", S[:], st_S.rearrange("h k v -> k h v"), [], [b_S])
            MEMSET(rawT[:, :, 0:3], 0.0, [b_raw])
            for j in range(3):
                DMA("sp", rawT[:, 0:30, j], st_conv[j].rearrange("(c p) -> p c", p=128), [], [b_raw], slow=True)
        CP(Sb[:], S[:], [b_S], [b_Sb])

        def heads_banks():
            (pa, pab), (pb_, pbb) = pbank(), pbank()
            return [(pa, pab), (pb_, pbb)]

        for ci in range(nchunks):
            c0 = ci * Cc
            DMA("sp", rawT[:, :, 3:3 + Cc], PJT[s][0:5120, c0:c0 + Cc].rearrange("(c p) t -> p c t", p=128),
                [b_PJT[s]], [b_raw])
            DMA("sp", abt[:], ABTM[s][c0:c0 + Cc, :], [b_AB[s]], [b_abt])
            for j in range(4):
                w_bc = gcw[:, j, :].unsqueeze(2).to_broadcast([128, 30, Cc])
                if j == 0:
                    TT_(acc[:], rawT[:, 0:30, 0:Cc], w_bc, ALU.mult, [b_raw, b_const], [b_acc])
                else:
                    TT_(tmp[:], rawT[:, 0:30, j:j + Cc], w_bc, ALU.mult, [b_raw, b_const], [b_tmp])
                    TT_(acc[:], acc[:], tmp[:], ALU.add, [b_acc, b_tmp], [b_acc])
            ACT(acc[:], acc[:], AF.Silu, [b_acc], [b_acc])
            ACT(sz[:], rawT[:, 30:40, 3:3 + Cc], AF.Silu, [b_raw], [b_sz])
            if ci == nchunks - 1:
                for j in range(3):
                    DMA("sp", o_conv[s][j].rearrange("(c p) -> p c", p=128), rawT[:, 0:30, Cc + j], [b_raw], [b_out], slow=True)
            else:
                CP(tmp[:, :, 0:3], rawT[:, 0:30, Cc:Cc + 3], [b_raw], [b_tmp])
                CP(rawT[:, 0:30, 0:3], tmp[:, :, 0:3], [b_tmp], [b_raw])
            ACT(tmp[:, 0:20, :], acc[:, 0:20, :], AF.Square, [b_acc], [b_tmp])
            tmpf = tmp[:, 0:20, :].rearrange("p a b -> p (a b)")
            rnf = rn[:].rearrange("p a b -> p (a b)")
            ncol = 20 * Cc
            for o in range(0, ncol, 512):
                n = min(512, ncol - o)
                pb, pbb = pbank()
                MM(pb[:, 0:n], ones[:, :], tmpf[:, o:o + n], True, True, [b_tmp, b_const], [pbb])
                ACT(rnf[:, o:o + n], pb[:, 0:n], AF.Sqrt, [pbb, b_const], [b_rn], bias=epsT[:, 0:1])
            RECIP(rn[:], rn[:], [b_rn], [b_rn])
            STT(acc[:, 0:10, :], acc[:, 0:10, :], 128.0 ** -0.5, rn[:, 0:10, :], ALU.mult, ALU.mult, [b_acc, b_rn], [b_acc])
            TT_(acc[:, 10:20, :], acc[:, 10:20, :], rn[:, 10:20, :], ALU.mult, [b_acc, b_rn], [b_acc])
            ACP(qkb[:], acc[:, 0:20, :], [b_acc], [b_qkb])
            xa, t_e, t_r, gg, bt, gcol, ekd = (sm[:, i, :] for i in range(7))
            TT_(xa, abt[:, 0:10], dtb[0:Cc, :], ALU.add, [b_abt, b_const], [b_sm])
            ACT(t_e, xa, AF.Abs, [b_sm], [b_sm])
            ACT(t_e, t_e, AF.Exp, [b_sm], [b_sm], scale=-1.0)
            ACT(t_e, t_e, AF.Ln, [b_sm, b_const], [b_sm], bias=ones[0:Cc, 0:1])
            TS(t_r, xa, 0.0, None, ALU.max, None, [b_sm], [b_sm])
            TT_(t_r, t_r, t_e, ALU.add, [b_sm], [b_sm])
            TT_(gg, t_r, nea[0:Cc, :], ALU.mult, [b_sm, b_const], [b_sm])
            ACT(bt, abt[:, 10:20], AF.Sigmoid, [b_abt], [b_sm])
            TT_(rhsg[:], triU[0:Cc, 0:Cc].unsqueeze(1).to_broadcast([Cc, 10, Cc]), gg.unsqueeze(2).to_broadcast([Cc, 10, Cc]),
                ALU.mult, [b_const, b_sm], [b_rhsg])
            TT_(rhsb[:], ident[0:Cc, 0:Cc].unsqueeze(1).to_broadcast([Cc, 10, Cc]), bt.unsqueeze(2).to_broadcast([Cc, 10, Cc]),
                ALU.mult, [b_const, b_sm], [b_rhsb])
            pb, pbb = pbank()
            MM(pb[0:Cc, 0:10], triU[0:Cc, 0:Cc], gg, True, True, [b_const, b_sm], [pbb])
            CP(gcol, pb[0:Cc, 0:10], [pbb], [b_sm])
            for (src, sb_, dst, db_) in ((rhsg, b_rhsg, grow, b_grow), (rhsb, b_rhsb, brow, b_brow)):
                for hb in range(2):
                    pb, pbb = pbank()
                    MM(pb[:, 0:NB5], ones[0:Cc, :], src[:, hb * 5:hb * 5 + 5, :].rearrange("p a b -> p (a b)"), True, True,
                       [sb_, b_const], [pbb])
                    ACP(dst[:, hb * 5:hb * 5 + 5, :].rearrange("p a b -> p (a b)"), pb[:, 0:NB5], [pbb], [db_])
            TT_(dd[:], grow[0:Cc, :, :], gcol.unsqueeze(2).to_broadcast([Cc, 10, Cc]), ALU.subtract, [b_grow, b_sm], [b_dd])
            TS(decT[:], dd[:], 0.0, None, ALU.min, None, [b_dd], [b_decT])
            ACT(decT[:], decT[:], AF.Exp, [b_decT], [b_decT])
            TS(decN[:], dd[:], 0.0, -1.0, ALU.max, ALU.mult, [b_dd], [b_decN])
            ACT(decN[:], decN[:], AF.Exp, [b_decN], [b_decN])
            kkb = heads_banks()
            for h in range(10):
                pb, pbb = kkb[h // 5]
                o = (h % 5) * Cc
                MM(pb[0:Cc, o:o + Cc], qkb[:, 10 + h, :], qkb[:, 10 + h, :], True, True, [b_qkb], [pbb])
            qkbk = heads_banks()
            for h in range(10):
                pb, pbb = qkbk[h // 5]
                o = (h % 5) * Cc
                MM(pb[0:Cc, o:o + Cc], qkb[:, 10 + h, :], qkb[:, h, :], True, True, [b_qkb], [pbb])
            mU = triU[0:Cc, 0:Cc].unsqueeze(1).to_broadcast([Cc, 10, Cc])
            mUs = triUs[0:Cc, 0:Cc].unsqueeze(1).to_broadcast([Cc, 10, Cc])
            mLs = triLs[0:Cc, 0:Cc].unsqueeze(1).to_broadcast([Cc, 10, Cc])
            idb = ident[0:Cc, 0:Cc].unsqueeze(1).to_broadcast([Cc, 10, Cc])
            TT_(t1[:], decT[:], mUs, ALU.mult, [b_decT, b_const], [b_t1])
            TT_(t1[:], t1[:], brow[0:Cc, :, :], ALU.mult, [b_t1, b_brow], [b_t1])
            TT_(t2[:], decN[:], mLs, ALU.mult, [b_decN, b_const], [b_t2])
            TT_(t2[:], t2[:], bt.unsqueeze(2).to_broadcast([Cc, 10, Cc]), ALU.mult, [b_t2, b_sm], [b_t2])
            TT_(t3[:], decT[:], mU, ALU.mult, [b_decT, b_const], [b_t3])
            (N0, b_N0), (B0, b_B0) = Nk[0], Bk[0]
            for hb in range(2):
                pb, pbb = kkb[hb]
                kv_ = pb[0:Cc, 0:NB5].rearrange("p (a b) -> p a b", b=Cc)
                STT(N0[:, hb * 5:hb * 5 + 5, :], kv_, -1.0, t1[:, hb * 5:hb * 5 + 5, :], ALU.mult, ALU.mult, [pbb, b_t1], [b_N0])
                STT(B0[:, hb * 5:hb * 5 + 5, :], kv_, -1.0, t2[:, hb * 5:hb * 5 + 5, :], ALU.mult, ALU.mult, [pbb, b_t2], [b_B0])
                pb, pbb = qkbk[hb]
                qv_ = pb[0:Cc, 0:NB5].rearrange("p (a b) -> p a b", b=Cc)
                TT_(QKT[:, hb * 5:hb * 5 + 5, :], qv_, t3[:, hb * 5:hb * 5 + 5, :], ALU.mult, [pbb, b_t3], [b_QKT])
            levels = [bsz for bsz in (1, 2, 4, 8, 16, 32) if bsz < Cc]
            (Wc, b_Wc), (Wtc, b_Wtc) = Wk[0], Wtk[0]
            CP(Wc[:], idb, [b_const], [b_Wc])
            CP(Wtc[:], idb, [b_const], [b_Wtc])
            cur = 0
            for li, bsz in enumerate(levels):
                lidx = (1, 2, 4, 8, 16, 32).index(bsz)
                (Wc, b_Wc), (Wtc, b_Wtc) = Wk[cur], Wtk[cur]
                (Wn, b_Wn), (Wtn, b_Wtn) = Wk[1 - cur], Wtk[1 - cur]
                last = li == len(levels) - 1
                mN = lmask[0:Cc, lidx, 0:Cc].unsqueeze(1).to_broadcast([Cc, 10, Cc])
                mT = lmask[0:Cc, 6 + lidx, 0:Cc].unsqueeze(1).to_broadcast([Cc, 10, Cc])
                TT_(Boff[:], B0[:], mN, ALU.mult, [b_B0, b_const], [b_Boff])
                TT_(Noff[:], N0[:], mT, ALU.mult, [b_N0, b_const], [b_Noff])
                if not last:
                    xb = heads_banks()
                    for h in range(10):
                        pb, pbb = xb[h // 5]
                        o = (h % 5) * Cc
                        MM(pb[0:Cc, o:o + Cc], Noff[:, h, :], Wc[:, h, :], True, True, [b_Noff, b_Wc], [pbb])
                    for hb in range(2):
                        pb, pbb = xb[hb]
                        ACP(Xs[:, hb * 5:hb * 5 + 5, :].rearrange("p a b -> p (a b)"), pb[0:Cc, 0:NB5], [pbb], [b_Xs])
                    yb = heads_banks()
                    for h in range(10):
                        pb, pbb = yb[h // 5]
                        o = (h % 5) * Cc
                        MM(pb[0:Cc, o:o + Cc], Wtc[:, h, :], Xs[:, h, :], True, True, [b_Wtc, b_Xs], [pbb])
                    for hb in range(2):
                        pb, pbb = yb[hb]
                        TT_(Wn[:, hb * 5:hb * 5 + 5, :].rearrange("p a b -> p (a b)"), pb[0:Cc, 0:NB5],
                            Wc[:, hb * 5:hb * 5 + 5, :].rearrange("p a b -> p (a b)"), ALU.add, [pbb, b_Wc], [b_Wn])
                xb = heads_banks()
                for h in range(10):
                    pb, pbb = xb[h // 5]
                    o = (h % 5) * Cc
                    MM(pb[0:Cc, o:o + Cc], Boff[:, h, :], Wtc[:, h, :], True, True, [b_Boff, b_Wtc], [pbb])
                for hb in range(2):
                    pb, pbb = xb[hb]
                    ACP(X2s[:, hb * 5:hb * 5 + 5, :].rearrange("p a b -> p (a b)"), pb[0:Cc, 0:NB5], [pbb], [b_X2s])
                yb = heads_banks()
                for h in range(10):
                    pb, pbb = yb[h // 5]
                    o = (h % 5) * Cc
                    MM(pb[0:Cc, o:o + Cc], Wc[:, h, :], X2s[:, h, :], True, True, [b_Wc, b_X2s], [pbb])
                for hb in range(2):
                    pb, pbb = yb[hb]
                    TT_(Wtn[:, hb * 5:hb * 5 + 5, :].rearrange("p a b -> p (a b)"), pb[0:Cc, 0:NB5],
                        Wtc[:, hb * 5:hb * 5 + 5, :].rearrange("p a b -> p (a b)"), ALU.add, [pbb, b_Wtc], [b_Wtn])
                cur = 1 - cur
            Pf, b_Pf = Wtk[cur]
            ACT(egr[:], grow[:], AF.Exp, [b_grow], [b_egr])
            TT_(qdT[:], acc[:, 0:10, :], egr[:], ALU.mult, [b_acc, b_egr], [b_qdT])
            TT_(bge[:], brow[:], egr[:], ALU.mult, [b_brow, b_egr], [b_bge])
            TT_(kbgT[:], acc[:, 10:20, :], bge[:], ALU.mult, [b_acc, b_bge], [b_kbgT])
            TT_(ekd, grow[0:Cc, :, Cc - 1], gcol, ALU.subtract, [b_grow, b_sm], [b_sm])
            ACT(ekd, ekd, AF.Exp, [b_sm], [b_sm])
            for (srcoff, scal, dst, db_) in ((10, ekd, kd, b_kd), (20, bt, vb, b_vb)):
                for h0 in range(0, 10, 4):
                    nh = min(4, 10 - h0)
                    pb, pbb = pbank()
                    for hh in range(nh):
                        TR(pb[0:Cc, hh * 128:(hh + 1) * 128], acc[:, srcoff + h0 + hh, :], ident[:, :], [b_acc, b_const], [pbb])
                    TT_(dst[:, h0:h0 + nh, :], pb[0:Cc, 0:nh * 128].rearrange("p (a b) -> p a b", b=128),
                        scal[:, h0:h0 + nh].unsqueeze(2).to_broadcast([Cc, nh, 128]), ALU.mult, [pbb, b_sm], [db_])
            for h0 in range(0, 10, 4):
                nh = min(4, 10 - h0)
                pb, pbb = pbank()
                for hh in range(nh):
                    MM(pb[0:Cc, hh * 128:(hh + 1) * 128], kbgT[:, h0 + hh, :], Sb[:, h0 + hh, :], True, True, [b_kbgT, b_Sb], [pbb])
                TT_(rb[:, h0:h0 + nh, :], vb[:, h0:h0 + nh, :], pb[0:Cc, 0:nh * 128].rearrange("p (a b) -> p a b", b=128),
                    ALU.subtract, [pbb, b_vb], [b_rb])
            for h0 in range(0, 10, 4):
                nh = min(4, 10 - h0)
                pb, pbb = pbank()
                for hh in range(nh):
                    MM(pb[0:Cc, hh * 128:(hh + 1) * 128], Pf[:, h0 + hh, :], rb[:, h0 + hh, :], True, True, [b_Pf, b_rb], [pbb])
                ACP(ub[:, h0:h0 + nh, :].rearrange("p a b -> p (a b)"), pb[0:Cc, 0:nh * 128], [pbb], [b_ub])
            ob_ = heads_banks()
            for h in range(10):
                pb, pbb = ob_[h // 5]
                o = (h % 5) * Cc
                MM(pb[:, o:o + Cc], Sb[:, h, :], qdT[:, h, :], True, False, [b_Sb, b_qdT], [pbb])
                MM(pb[:, o:o + Cc], ub[:, h, :], QKT[:, h, :], False, True, [b_ub, b_QKT], [pbb])
            for hb in range(2):
                pb, pbb = ob_[hb]
                ACP(oT[:, hb * 5:hb * 5 + 5, :].rearrange("p a b -> p (a b)"), pb[:, 0:NB5], [pbb], [b_oT])
            TT_(S[:], S[:], egr[:, :, Cc - 1].unsqueeze(2).to_broadcast([128, 10, 128]), ALU.mult, [b_S, b_egr], [b_S])
            for h0 in range(0, 10, 4):
                nh = min(4, 10 - h0)
                pb, pbb = pbank()
                for hh in range(nh):
                    MM(pb[:, hh * 128:(hh + 1) * 128], kd[:, h0 + hh, :], ub[:, h0 + hh, :], True, True, [b_kd, b_ub], [pbb])
                TT_(S[:, h0:h0 + nh, :], S[:, h0:h0 + nh, :], pb[:, 0:nh * 128].rearrange("p (a b) -> p a b", b=128),
                    ALU.add, [pbb, b_S], [b_S])
            ACP(Sb[:], S[:], [b_S], [b_Sb])
            ACT(o2[:], oT[:], AF.Square, [b_oT], [b_o2])
            for hb in range(2):
                pb, pbb = pbank()
                MM(pb[:, 0:NB5], ones[:, :], o2[:, hb * 5:hb * 5 + 5, :].rearrange("p a b -> p (a b)"), True, True,
                   [b_o2, b_const], [pbb])
                ACT(bge[:, hb * 5:hb * 5 + 5, :].rearrange("p a b -> p (a b)"), pb[:, 0:NB5], AF.Sqrt, [pbb, b_const], [b_bge],
                    scale=1.0 / 128, bias=epsT[:, 0:1])
            RECIP(bge[:], bge[:], [b_bge], [b_bge])
            TT_(o2[:], oT[:], bge[:], ALU.mult, [b_oT, b_bge], [b_o2])
            STT(mixa[:], o2[:], onrm[:, 0:1], sz[:], ALU.mult, ALU.mult, [b_o2, b_sz, b_const], [b_mixa])
            DMA("sp", MIXT[s][0:1280, c0:c0 + Cc].rearrange("(h p) t -> p h t", p=128), mixa[:], [b_mixa], [b_MIX[s]])
        DMA("sp", o_S[s].rearrange("h k v -> k h v"), S[:], [b_S], [b_out])

    def kv_outputs(s):
        for g, (win, dil) in enumerate(SWA_CFG):
            for t in range(2):
                src_new = KVTM[s][:, t * 768 + g * 256:t * 768 + (g + 1) * 256].rearrange("l (h c) -> l h c", c=64)
                if s == 0:
                    DMA("sp", o_kv[s][g][:, t, :, :], src_new[LP - win:LP], [b_KV[s]], [b_out])
                else:
                    DMA("sp", o_kv[s][g][0:win - LS, t, :, :], st_kv[g][LS:win, t, :, :], [], [b_out])
                    DMA("sp", o_kv[s][g][win - LS:win, t, :, :], src_new[0:LS], [b_KV[s]], [b_out])

    def swa_prompt():
        s = 0
        Lq = L[0]
        qf, b_qf = carve([64, Lq], F32)
        kf, b_kf = carve([64, Lq], F32)
        qb_, b_qb = carve([64, Lq], BF16)
        kb_, b_kb = carve([64, Lq], BF16)
        NBT = Lq // 128
        vf, b_vf = carve([128, NBT, 64], F32)
        vaug, b_vaug = carve([128, NBT, 65], BF16)
        Pm = [carve([128, 2, 128], BF16) for _ in range(2)]
        msk, b_msk = carve([128, 2, 128], F32)
        oacc, b_oacc = carve([128, NBT, 65], F32)
        CP(msk[:, 0, :], triU, [b_const], [b_msk])
        CP(msk[:, 1, :], triL, [b_const], [b_msk])
        MEMSET(vaug[:, :, 64:65], 1.0, [b_vaug])
        it = 0
        for g, (win, d) in enumerate(SWA_CFG):
            n = Lq // d
            NB = n // 128
            if NB == 0:
                continue
            for hh in range(4):
                hb = g * 4 + hh
                DMA("sp", qf[:], PJT[s][5120 + hb * 64:5120 + hb * 64 + 64, 0:Lq], [b_PJT[s]], [b_qf])
                DMA("sp", kf[:], PJT[s][5888 + hb * 64:5888 + hb * 64 + 64, 0:Lq], [b_PJT[s]], [b_kf])
                TS(qb_[:].rearrange("p (r u) -> p r u", r=d), qf[:].rearrange("p (u r) -> p r u", r=d), 0.125, None,
                   ALU.mult, None, [b_qf], [b_qb])
                ACP(kb_[:].rearrange("p (r u) -> p r u", r=d), kf[:].rearrange("p (u r) -> p r u", r=d), [b_kf], [b_kb])
                vsrc = KVTM[s][0:Lq, 768 + hb * 64:768 + hb * 64 + 64].rearrange("(b j r) c -> j r b c", j=128, r=d)
                for r in range(d):
                    DMA("sp", vf[:, r * NB:(r + 1) * NB, :], vsrc[:, r, :, :], [b_KV[s]], [b_vf])
                CP(vaug[:, :, 0:64], vf[:], [b_vf], [b_vaug])
                for r in range(d):
                    for b in range(NB):
                        blk = r * NB + b
                        col = r * n + b * 128
                        P_, b_P = Pm[it % 2]
                        it += 1
                        pb, pbb = pbank()
                        nk = 2 if b > 0 else 1
                        MM(pb[:, 0:128], kb_[:, col:col + 128], qb_[:, col:col + 128], True, True, [b_kb, b_qb], [pbb])
                        if b > 0:
                            MM(pb[:, 128:256], kb_[:, col - 128:col], qb_[:, col:col + 128], True, True, [b_kb, b_qb], [pbb])
                        ACT(P_[:, 0:nk, :].rearrange("p a b -> p (a b)"), pb[:, 0:nk * 128], AF.Exp, [pbb], [b_P])
                        TT_(P_[:, 0:nk, :], P_[:, 0:nk, :], msk[:, 0:nk, :], ALU.mult, [b_P, b_msk], [b_P])
                        po, pob = pbank()
                        MM(po[:, 0:65], P_[:, 0, :], vaug[:, blk, :], True, b == 0, [b_P, b_vaug], [pob])
                        if b > 0:
                            MM(po[:, 0:65], P_[:, 1, :], vaug[:, blk - 1, :], False, True, [b_P, b_vaug], [pob])
                        ACP(oacc[:, blk, :], po[:, 0:65], [pob], [b_oacc])
                dst = OB[s][0:Lq, hb, :].rearrange("(b j r) c -> j r b c", j=128, r=d)
                for r in range(d):
                    DMA("sp", dst[:, r, :, :], oacc[:, r * NB:(r + 1) * NB, :], [b_oacc], [b_OB[s]])

    def swa_sample():
        s = 1
        qs, b_qs = carve([64, 12, LS], F32)
        ks, b_ks = carve([64, 12, LS], F32)
        qsb, b_qsb = carve([64, 12, LS], BF16)
        ksb, b_ksb = carve([64, 12, LS], BF16)
        vn, b_vn = carve([LS, 12, 64], F32)
        vnaug, b_vnaug = carve([LS, 12, 65], BF16)
        past, b_past = carve([128, 4, 2, 4, 64], F32) if False else (None, None)
        pk, b_pk = carve([128, 4, 512], F32)
        pvaug, b_pvaug = carve([128, 4, 4, 65], BF16)
        kTb, b_kTb = carve([64, 128], BF16)
        Pp, b_Pp = carve([128, LS], BF16)
        Pn_, b_Pn = carve([LS, LS], BF16)
        osb, b_osb = carve([LS, 12, 65], F32)
        DMA("sp", qs[:], PJT[s][5120:5888, :].rearrange("(h p) t -> p h t", p=64), [b_PJT[s]], [b_qs])
        DMA("sp", ks[:], PJT[s][5888:6656, :].rearrange("(h p) t -> p h t", p=64), [b_PJT[s]], [b_ks])
        TS(qsb[:], qs[:], 0.125, None, ALU.mult, None, [b_qs], [b_qsb])
        CP(ksb[:], ks[:], [b_ks], [b_ksb])
        DMA("sp", vn[:], KVTM[s][:, 768:1536].rearrange("l (h c) -> l h c", c=64), [b_KV[s]], [b_vn])
        MEMSET(vnaug[:, :, 64:65], 1.0, [b_vnaug])
        CP(vnaug[:, :, 0:64], vn[:], [b_vn], [b_vnaug])
        MEMSET(pvaug[:, :, :, 64:65], 1.0, [b_pvaug])
        for g, (win, d) in enumerate(SWA_CFG):
            X = min(d, 4)
            src = st_kv[g].rearrange("(m x) t h c -> m x (t h c)", x=d)
            for x in range(X):
                DMA("sp", pk[:, x, :], src[:, x, :], [], [b_pk])
            for x in range(X):
                CP(pvaug[:, x, :, 0:64], pk[:, x, 256:512].rearrange("p (h c) -> p h c", c=64), [b_pk], [b_pvaug])
            for hh in range(4):
                hb = g * 4 + hh
                po, pob = ps[7], ps_b[7]
                for x in range(X):
                    pt, ptb = ps[x % 2], ps_b[x % 2]
                    TR(pt[0:64, 0:128], pk[:, x, hh * 64:(hh + 1) * 64], ident[:, :], [b_pk, b_const], [ptb])
                    CP(kTb[:], pt[0:64, 0:128], [ptb], [b_kTb])
                    pq, pqb = ps[2 + x % 2], ps_b[2 + x % 2]
                    MM(pq[:, 0:LS], kTb[:], qsb[:, hb, :], True, True, [b_kTb, b_qsb], [pqb])
                    ACT(Pp[:], pq[:, 0:LS], AF.Exp, [pqb], [b_Pp])
                    mi = 0 if g == 0 else 1 + x
                    TT_(Pp[:], Pp[:], smask[:, mi, :], ALU.mult, [b_Pp, b_const], [b_Pp])
                    MM(po[0:LS, 0:65], Pp[:], pvaug[:, x, hh, :], x == 0, False, [b_Pp, b_pvaug], [pob])
                pq, pqb = ps[4], ps_b[4]
                MM(pq[0:LS, 0:LS], ksb[:, hb, :], qsb[:, hb, :], True, True, [b_ksb, b_qsb], [pqb])
                ACT(Pn_[:], pq[0:LS, 0:LS], AF.Exp, [pqb], [b_Pn])
                nm = smask[0:LS, 5, :] if g == 0 else ident[0:LS, 0:LS]
                TT_(Pn_[:], Pn_[:], nm, ALU.mult, [b_Pn, b_const], [b_Pn])
                MM(po[0:LS, 0:65], Pn_[:], vnaug[:, hb, :], False, True, [b_Pn, b_vnaug], [pob])
                CP(osb[:, hb, :], po[0:LS, 0:65], [pob], [b_osb])
        DMA("sp", OB[s][:, :, :], osb[:], [b_osb], [b_OB[s]])

    def swa_finish(s):
        ob, b_ob = carve([128, 12, 65], F32)
        den, b_den = carve([128, 4], F32)
        obn, b_obn = carve([128, 12, 64], F32)
        obT, b_obT = carve([128, 6, 128], BF16)
        Ls = L[s]
        for t0 in range(0, Ls, 128):
            nt = min(128, Ls - t0)
            DMA("sp", ob[0:nt], OB[s][t0:t0 + nt, :, :], [b_OB[s]], [b_ob])
            TT_(den[0:nt], ob[0:nt, 0:4, 64], ob[0:nt, 4:8, 64], ALU.add, [b_ob], [b_den])
            TT_(den[0:nt], den[0:nt], ob[0:nt, 8:12, 64], ALU.add, [b_ob, b_den], [b_den])
            RECIP(den[0:nt], den[0:nt], [b_den], [b_den])
            for g in range(3):
                TT_(obn[0:nt, g * 4:(g + 1) * 4, :], ob[0:nt, g * 4:(g + 1) * 4, 0:64],
                    den[0:nt].unsqueeze(2).to_broadcast([nt, 4, 64]), ALU.mult, [b_ob, b_den], [b_obn])
            obnf = obn[:].rearrange("p a b -> p (a b)")
            for half in range(2):
                pb, pbb = pbank()
                for k in range(3):
                    kc = half * 3 + k
                    TR(pb[:, k * 128:k * 128 + nt], obnf[0:nt, kc * 128:(kc + 1) * 128], ident[0:nt, 0:nt], [b_obn, b_const], [pbb])
                CP(obT[:, half * 3:half * 3 + 3, 0:nt], pb[:, 0:384].rearrange("p (k t) -> p k t", t=128)[:, :, 0:nt], [pbb], [b_obT])
            DMA("sp", MIXT[s][1280:2048, t0:t0 + nt].rearrange("(k p) t -> p k t", p=128), obT[:, :, 0:nt], [b_obT], [b_MIX[s]])

    for s in range(2):
        stage_reset()
        gdn(s)
        if L[0] == LP:
            kv_outputs(s)
    stage_reset()
    swa_prompt()
    stage_reset()
    swa_sample()
    stage_reset()
    for s in range(2):
        swa_finish(s)

    if "stopB" in dbg:
        fw.emit(final_bufs=list(fw.allbufs))
        return nc

    stage_reset()
    mtc = carve_tile()
    mtc.cx, mtc.b_cx = carve([128, KD, 2 + TT], F32)
    rtc = make_rider()
    ntl = L[0] // TT
    for tl in range(ntl):
        hT, b_hT = mtc.hT, mtc.b_hT
        DMA("sp", hT[:], MIXT[0].rearrange("(k p) t -> p k t", p=128)[:, :, tl * TT:(tl + 1) * TT], [b_MIX[0]], list(b_hT))
        DMA("sp", MIXB[tl * 128:(tl + 1) * 128, :], hT[:].rearrange("p k t -> p (k t)"), list(b_hT), [b_MIXB])
        DMA("sp", MIXH[tl * 128:(tl + 1) * 128, :].rearrange("p (k t) -> p k t", t=2), hT[:, :, TT - 2:TT], list(b_hT), [b_MIXB])
    nown = min(2, ntl)
    groups8 = [[(m * 256, 256)] for m in range(8)]
    for ti_ in range(nown):
        mtc.T, mtc.segs, mtc.kind = TT, [(0, 0, TT)], "own"
        tcs = [mtc]
        GATHER(mtc.xT[:].rearrange("p k t -> p (k t)"), X1B, ti_, [b_X1B], list(mtc.b_xT))
        GATHER(mtc.hT[:].rearrange("p k t -> p (k t)"), MIXB, ti_, [b_MIXB], list(mtc.b_hT))
        if ti_ == 0:
            rtc.T, rtc.segs, rtc.kind = LS + 2, [(1, 0, LS), (0, LS, 2)], "rider"
            tcs.append(rtc)
            DMA("sp", rtc.xT[:, :, 0:LS], X1T[1].rearrange("(k p) t -> p k t", p=128)[:, :, 0:LS], [b_X1T[1]], list(rtc.b_xT))
            DMA("sp", rtc.hT[:, :, 0:LS], MIXT[1].rearrange("(k p) t -> p k t", p=128)[:, :, 0:LS], [b_MIX[1]], list(rtc.b_hT))
            GATHER(hst[:, :], X1H, 2, [b_X1B], [b_hst])
            CP(rtc.xT[:, :, LS:LS + 2], hst[:, :].rearrange("p (k t) -> p k t", t=2), [b_hst], list(rtc.b_xT))
            GATHER(hstb[:, :], MIXH, 2, [b_MIXB], [b_hst])
            CP(rtc.hT[:, :, LS:LS + 2], hstb[:, :].rearrange("p (k t) -> p k t", t=2), [b_hst], list(rtc.b_hT))
            for j in range(2):
                DMA("sp", rtc.cx[:, :, j], st_sc[j].rearrange("(k p) -> p k", p=128), [], [rtc.b_cx], slow=True)

        def lin_resid(W, l, j, src):
            def mk(tc):
                T = tc.T
                use(tc)
                buf = tc.hT if src == "h" else tc.aT
                bb = (lambda kc: [tc.b_hT[kc]]) if src == "h" else [tc.b_aT]
                return (lambda kc: buf[:, kc, 0:T]), bb, T, resid_evac(l, j, tc.segs, T)
            parts = [mk(tc) for tc in tcs]
            linear_fm(W, groups8, [(0, KD)], *parts[0], extra=parts[1:])

        def all_norm(l, j):
            for tc in tcs:
                use(tc)
                norm_mod(l, j, tc.segs, tc.T)

        lin_resid(w_out0, 0, 1, "h")
        all_norm(0, 2)
        ffn(0, 1, 2, tcs)
        all_norm(1, 0)
        ffn(1, 0, 0, tcs)
        all_norm(1, 1)
        groups = [[(D + m * 256, 256)] for m in range(KD)]

        def mk_cx(tc):
            T = tc.T

            def ev_cx(gi, ci, rows, p_ap, p_b):
                bi = gi % 2
                if ci == 0:
                    ACP(tc.sg[bi][:, 0:T], p_ap, [p_b], [tc.sg_b[bi]])
                else:
                    TT_(tc.cx[:, gi, 2:2 + T], p_ap, tc.sg[bi][:, 0:T], ALU.mult, [p_b, tc.sg_b[bi]], [tc.b_cx])
            return (lambda kc: tc.hT[:, kc, 0:T]), (lambda kc: [tc.b_hT[kc]]), T, ev_cx
        parts = [mk_cx(tc) for tc in tcs]
        linear_fm(sc_w_in, groups, [(0, KD)], *parts[0], extra=parts[1:])
        if ti_ == 0:
            TS(mtc.cx[:, :, 0:2], rtc.cx[:, :, 2 + LS:2 + LS + 2], hflag[:, 0:1], None, ALU.mult, None,
               [rtc.b_cx, b_const], [mtc.b_cx])

        def mk_bg(tc):
            T = tc.T
            cx, b_cx, ytmp, b_ytmp = tc.cx, tc.b_cx, tc.yt, tc.b_yt

            def ev_bg(gi, ci, rows, p_ap, p_b):
                dch = gi * 2 + ci
                TS(ytmp[:, 0:T], cx[:, dch, 0:T], scw[:, 0, dch:dch + 1], None, ALU.mult, None, [b_cx, b_const], [b_ytmp])
                STT(ytmp[:, 0:T], cx[:, dch, 1:1 + T], scw[:, 1, dch:dch + 1], ytmp[:, 0:T], ALU.mult, ALU.add,
                    [b_cx, b_const, b_ytmp], [b_ytmp])
                STT(ytmp[:, 0:T], cx[:, dch, 2:2 + T], scw[:, 2, dch:dch + 1], ytmp[:, 0:T], ALU.mult, ALU.add,
                    [b_cx, b_const, b_ytmp], [b_ytmp])
                TT_(tc.aT[:, dch, 0:T], p_ap, ytmp[:, 0:T], ALU.mult, [p_b, b_ytmp], [tc.b_aT])
            return (lambda kc: tc.hT[:, kc, 0:T]), (lambda kc: [tc.b_hT[kc]]), T, ev_bg
        parts = [mk_bg(tc) for tc in tcs]
        linear_fm(sc_w_in, groups8, [(0, KD)], *parts[0], extra=parts[1:])
        if ti_ == 0:
            for j in range(2):
                DMA("sp", o_sc[1][j].rearrange("(k p) -> p k", p=128), rtc.cx[:, :, LS + j], [rtc.b_cx], [b_out], slow=True)
        if ti_ == nown - 1:
            for j in range(2):
                DMA("sp", o_sc[0][j].rearrange("(k p) -> p k", p=128), mtc.cx[:, :, TT + j], [mtc.b_cx], [b_out], slow=True)
        else:
            CP(sg[0][:, 0:32].rearrange("p (k j) -> p k j", j=2), mtc.cx[:, :, TT:TT + 2], [mtc.b_cx], [sg_b[0]])
            CP(mtc.cx[:, :, 0:2], sg[0][:, 0:32].rearrange("p (k j) -> p k j", j=2), [sg_b[0]], [mtc.b_cx])
        lin_resid(sc_w_out, 1, 1, "a")
        all_norm(1, 2)
        ffn(1, 1, 2, tcs)
        for tc in tcs:
            use(tc)
            xT, b_xT, T = tc.xT, tc.b_xT, tc.T
            rms_stats(T, D)
            for k in range(KD):
                STT(xT[:, k, 0:T], xT[:, k, 0:T], lnT[:, 6, k:k + 1], rstd[:, 0:T], ALU.mult, ALU.mult,
                    [b_xT[k], b_rstd, b_const], [b_xT[k]])
            To = LS if tc.kind == "rider" else T
            so = 1 if tc.kind == "rider" else 0
            r0 = 0 if tc.kind == "rider" else ti_ * TT
            ntb = (To + 127) // 128
            for tb in range(ntb):
                n = min(128, To - tb * 128)
                bi = tb % 2
                for k4 in range(4):
                    pb, pbb = misc_bank()
                    for kk in range(4):
                        k = k4 * 4 + kk
                        TR(pb[0:n, kk * 128:(kk + 1) * 128], xT[:, k, tb * 128:tb * 128 + n], ident[:, :], [b_xT[k], b_const], [pbb])
                    CP(xtm[bi][0:n, k4 * 512:(k4 + 1) * 512], pb[0:n, :], [pbb], [xtm_b[bi]])
                DMA("sp", yout[so][r0 + tb * 128:r0 + tb * 128 + n, :], xtm[bi][0:n, :], [xtm_b[bi]], [b_out])

    fw.emit(final_bufs=list(fw.allbufs))
    return nc


def make_consts():
    c = np.zeros((128, NCONST), np.float32)
    j = np.arange(128)[:, None]
    i = np.arange(128)[None, :]
    c[:, 0:128] = (j == i)
    c[:, 128:256] = (j <= i)
    c[:, 256:384] = (j < i)
    c[:, 384:512] = (j >= i)
    c[:, 512:640] = (j > i)
    sm = np.zeros((128, 6, 4), np.float32)
    q = np.arange(4)[None, :]
    sm[:, 0, :] = (np.arange(128)[:, None] >= q)
    for x in range(4):
        sm[:, 1 + x, x] = 1.0
    sm[0:4, 5, :] = (np.arange(4)[:, None] <= q)
    c[:, 640:664] = sm.reshape(128, 24)
    lm = np.zeros((128, 12, 64), np.float32)
    ii = np.arange(64)[:, None]
    jj = np.arange(64)[None, :]
    for li, bsz in enumerate((1, 2, 4, 8, 16, 32)):
        m = (((ii // bsz) % 2) == 1) & ((jj // bsz) == (ii // bsz) - 1)
        lm[0:64, li, :] = m
        lm[0:64, 6 + li, :] = m.T
    c[:, 664:1432] = lm.reshape(128, 768)
    return c


_CACHE = {}


def make_in_maps(inp):
    f = lambda a: np.ascontiguousarray(np.asarray(a, dtype=np.float32))
    lns = np.stack([inp["ln_ffn1"][0], inp["ln_mix"][0], inp["ln_ffn2"][0], inp["ln_ffn1"][1], inp["ln_mix"][1],
                    inp["ln_ffn2"][1], inp["ln_final"]]).astype(np.float32)
    wu = np.asarray(inp["ffn_w_up"], dtype=np.float32)
    wu = np.ascontiguousarray(np.stack([wu[..., :FF].reshape(2, 2, D, FC, 128), wu[..., FF:].reshape(2, 2, D, FC, 128)],
                                       axis=4).reshape(2, 2, D, 2 * FF))
    sw = np.asarray(inp["sc_w_in"][0], dtype=np.float32)
    sw = np.ascontiguousarray(np.concatenate(
        [sw[:, :D], np.stack([sw[:, D:2 * D].reshape(D, KD, 128), sw[:, 2 * D:].reshape(D, KD, 128)], axis=2).reshape(D, 2 * D)],
        axis=1))
    shared = {"ada_w": f(inp["ada_w"]), "ada_b": f(inp["ada_b"]), "lns": lns, "ffn_w_up": wu,
              "ffn_w_down": f(inp["ffn_w_down"]), "w_in0": f(inp["w_in0"][0]), "gdn_conv_w": f(inp["gdn_conv_w"][0]),
              "gdn_a_log": f(inp["gdn_a_log"][0]), "gdn_dt_bias": f(inp["gdn_dt_bias"][0]), "gdn_onorm": f(inp["gdn_onorm"][0]),
              "w_out0": f(inp["w_out0"][0]), "sc_w_in": sw, "sc_conv_w": f(inp["sc_conv_w"][0]),
              "sc_w_out": f(inp["sc_w_out"][0]), "consts": make_consts()}
    in_maps = []
    for c in range(8):
        m = dict(shared)
        m["xp"] = f(inp["x_prompt"][c // 4])
        m["xs"] = f(inp["x_sample"][c])
        m["cvec"] = np.stack([inp["c_prompt"][c // 4], inp["c_sample"][c]]).astype(np.float32)
        m["st_S"] = f(inp["state_gdn_S"][0, c])
        m["st_conv"] = f(inp["state_gdn_conv"][0, c])
        for g in range(3):
            m[f"st_kv{g}"] = f(inp[f"cache_swa_kv_g{g}"][0, c])
        m["st_sc"] = f(inp["state_sconv"][0, c])
        r = c % 4
        p = np.arange(128)
        m["gidx"] = np.stack([2 * r * 128 + p, (2 * r + 1) * 128 + p, max(2 * r - 1, 0) * 128 + p], 1).astype(np.uint32)
        m["hflag"] = np.full((128, 1), 0.0 if r == 0 else 1.0, np.float32)
        in_maps.append(m)
    return in_maps


def kernel(**inp):
    if "nc" not in _CACHE:
        _CACHE["nc"] = build()
    nc = _CACHE["nc"]
    in_maps = make_in_maps(inp)
    res = run_bass_kernel_spmd(nc, in_maps, core_ids=list(range(8))).results
    pc = [0, 4]
    y_p = np.stack([np.concatenate([res[4 * b + r]["y_p"] for r in range(4)], 0) for b in range(2)])
    y_s = np.stack([res[c]["y_s"] for c in range(8)])
    outs = [y_p, y_s]
    for grp, cores in (("p", pc), ("s", list(range(8)))):
        outs.append(np.stack([res[c][f"o_S_{grp}"] for c in cores])[None])
        outs.append(np.stack([res[c][f"o_conv_{grp}"] for c in cores])[None])
        for g in range(3):
            outs.append(np.stack([res[c][f"o_kv{g}_{grp}"] for c in cores])[None])
        outs.append(np.stack([res[c + 3 if grp == "p" else c][f"o_sc_{grp}"] for c in cores])[None])
    return tuple(np.ascontiguousarray(o, dtype=np.float32) for o in outs)
```
